# Optimizing a Trainium2 kernel written in Bass

```python
import jax, jax.numpy as jnp
from jax import lax
import numpy as np

D_MODEL = 1024
BATCH = 32
SEQ = 2048
DEPTH = 1

HEAD_DIM = 64
RWKV_HEADS = 8
RWKV_WIDTH = RWKV_HEADS * HEAD_DIM
ATT_HEADS = 8
ATT_KV_HEADS = 2
ATT_GROUP = ATT_HEADS // ATT_KV_HEADS
ATT_WIDTH = ATT_HEADS * HEAD_DIM
ATT_KV_WIDTH = ATT_KV_HEADS * HEAD_DIM
DECAY_LORA = 64
ICLR_LORA = 64
N_DIR = 2
WINDOW = 128
BLOCK = 128
N_BRANCH = 2
NORM_EPS = 1e-6
GN_EPS = 64e-5

RWKV_COLS = 4 * RWKV_WIDTH + N_DIR * (DECAY_LORA + ICLR_LORA)
ATT_COLS = 2 * ATT_WIDTH + 2 * ATT_KV_WIDTH
GATE_COLS = N_BRANCH * D_MODEL
IN_COLS = RWKV_COLS + ATT_COLS + GATE_COLS

kernel_name = "hybrid_rwkv7_swa_gated_block"


def rms_norm(t, g):
    tf = t.astype(jnp.float32)
    tf = tf * lax.rsqrt(jnp.mean(tf * tf, axis=-1, keepdims=True) + NORM_EPS)
    return (tf * g.astype(jnp.float32)).astype(t.dtype)


def alibi_slopes(n_heads):
    return jnp.exp2(-8.0 * jnp.arange(1, n_heads + 1, dtype=jnp.float32) / n_heads)


def rwkv7_bidir(p, mu, w0, w_up, a0, a_up, k_k, k_a, r_k, gn_w, gn_b):
    B, S, _ = p.shape
    f32 = jnp.float32
    p = p.astype(f32)
    p_prev = jnp.pad(p, ((0, 0), (1, 0), (0, 0)))[:, :-1]
    p_next = jnp.pad(p, ((0, 0), (0, 1), (0, 0)))[:, 1:]
    mu = mu.astype(f32)
    p = p + mu[0] * (p_prev - p) + mu[1] * (p_next - p)
    W = RWKV_WIDTH
    r, k, v, g = jnp.split(p[..., :4 * W], 4, axis=-1)
    lo = p[..., 4 * W:]
    w_lo = lo[..., :N_DIR * DECAY_LORA].reshape(B, S, N_DIR, DECAY_LORA)
    a_lo = lo[..., N_DIR * DECAY_LORA:].reshape(B, S, N_DIR, ICLR_LORA)
    w_raw = w0.astype(f32)[:, None, None, :] + jnp.einsum('bsdr,drc->dbsc', jnp.tanh(w_lo), w_up.astype(f32))
    decay = jnp.exp(-jnp.exp(-jax.nn.softplus(-w_raw) - 0.5))
    a = jax.nn.sigmoid(a0.astype(f32)[:, None, None, :] + jnp.einsum('bsdr,drc->dbsc', a_lo, a_up.astype(f32)))
    kk = (k * k_k.astype(f32)).reshape(B, S, RWKV_HEADS, HEAD_DIM)
    kk = kk / jnp.maximum(jnp.sqrt(jnp.sum(kk * kk, axis=-1, keepdims=True)), 1e-12)
    k_dir = k[None] * (1.0 + (a - 1.0) * k_a.astype(f32))

    hs = lambda t: t.reshape(t.shape[:-1] + (RWKV_HEADS, HEAD_DIM))
    r_h, v_h = hs(r), hs(v)
    decay_h, a_h, k_h = hs(decay), hs(a), hs(k_dir)

    both = lambda t: jnp.stack([t, jnp.flip(t, axis=1)])
    order = lambda t: jnp.stack([t[0], jnp.flip(t[1], axis=1)])
    tm = lambda t: jnp.moveaxis(t, 2, 0)
    xs = (tm(both(r_h)), tm(order(decay_h)), tm(order(k_h)), tm(both(v_h)),
          tm(both(kk)), tm(order(a_h)))

    def step(state, inp):
        r_t, w_t, k_t, v_t, kk_t, a_t = inp
        sa = jnp.einsum('dbhij,dbhj->dbhi', state, kk_t)
        state = (state * w_t[..., None, :]
                 - sa[..., :, None] * (kk_t * a_t)[..., None, :]
                 + v_t[..., :, None] * k_t[..., None, :])
        y = jnp.einsum('dbhij,dbhj->dbhi', state, r_t)
        return state, y

    state0 = jnp.zeros((N_DIR, B, RWKV_HEADS, HEAD_DIM, HEAD_DIM), f32)
    _, ys = lax.scan(step, state0, xs)
    ys = jnp.moveaxis(ys, 0, 2)
    y = ys[0] + jnp.flip(ys[1], axis=1)
    mean = jnp.mean(y, axis=-1, keepdims=True)
    var = jnp.mean(jnp.square(y - mean), axis=-1, keepdims=True)
    y = (y - mean) * lax.rsqrt(var + GN_EPS)
    y = y * hs(gn_w.astype(f32)) + hs(gn_b.astype(f32))
    bonus = jnp.sum(jnp.sum(r_h[None] * k_h * r_k.astype(f32), axis=-1, keepdims=True) * v_h[None], axis=0)
    out = (y + bonus).reshape(B, S, W) * jax.nn.silu(g)
    return out


def windowed_gqa_sink(p, slopes, sink):
    B, S, _ = p.shape
    f32 = jnp.float32
    q = p[..., :ATT_WIDTH].reshape(B, S, ATT_KV_HEADS, ATT_GROUP, HEAD_DIM)
    k = p[..., ATT_WIDTH:ATT_WIDTH + ATT_KV_WIDTH].reshape(B, S, ATT_KV_HEADS, HEAD_DIM)
    v = p[..., ATT_WIDTH + ATT_KV_WIDTH:ATT_WIDTH + 2 * ATT_KV_WIDTH].reshape(B, S, ATT_KV_HEADS, HEAD_DIM)
    g = p[..., ATT_WIDTH + 2 * ATT_KV_WIDTH:]
    pad = ((0, 0), (BLOCK, BLOCK), (0, 0), (0, 0))
    k_pad = jnp.pad(k, pad)
    v_pad = jnp.pad(v, pad)
    n_blocks = S // BLOCK
    rel_q = jnp.arange(BLOCK)
    rel_k = jnp.arange(3 * BLOCK) - BLOCK
    dist = jnp.abs(rel_q[:, None] - rel_k[None, :])
    in_win = dist <= WINDOW
    bias = -slopes.reshape(ATT_KV_HEADS, ATT_GROUP)[:, :, None, None] * dist.astype(f32)
    sink_l = sink.astype(f32).reshape(ATT_KV_HEADS, ATT_GROUP)[None, :, :, None, None]
    scale = HEAD_DIM ** -0.5

    def block(i):
        start = i * BLOCK
        q_b = lax.dynamic_slice_in_dim(q, start, BLOCK, axis=1)
        k_b = lax.dynamic_slice_in_dim(k_pad, start, 3 * BLOCK, axis=1)
        v_b = lax.dynamic_slice_in_dim(v_pad, start, 3 * BLOCK, axis=1)
        kpos = start + rel_k
        valid = in_win & ((kpos >= 0) & (kpos < S))[None, :]
        s = jnp.einsum('bqkgd,bskd->bkgqs', q_b, k_b).astype(f32) * scale + bias
        s = jnp.where(valid, s, -1e30)
        sk = jnp.broadcast_to(sink_l, s.shape[:-1] + (1,))
        prob = jax.nn.softmax(jnp.concatenate([s, sk], axis=-1), axis=-1)[..., :-1]
        return jnp.einsum('bkgqs,bskd->bqkgd', prob.astype(v.dtype), v_b)

    o = lax.map(block, jnp.arange(n_blocks))
    o = jnp.moveaxis(o, 0, 1).reshape(B, S, ATT_WIDTH)
    return o * jax.nn.silu(g)


def setup_inputs(seed: int = 0) -> dict:
    key = jax.random.key(seed)
    ks = jax.random.split(key, 24)
    f32 = jnp.float32
    nrm = lambda k, shape, s: jax.random.normal(k, shape, f32) * s
    L, D, W = DEPTH, D_MODEL, RWKV_WIDTH
    return {
        "x": nrm(ks[0], (BATCH, SEQ, D), 1.0),
        "c": nrm(ks[1], (BATCH, D), 1.0),
        "w_ada": nrm(ks[2], (L, D, 3 * D), D ** -0.5),
        "b_ada": nrm(ks[3], (L, 3 * D), 0.01),
        "g_pre": 1.0 + nrm(ks[4], (L, D), 0.02),
        "g_post": 1.0 + nrm(ks[5], (L, D), 0.02),
        "w_in": nrm(ks[6], (L, D, IN_COLS), D ** -0.5),
        "mu_shift": jax.random.uniform(ks[7], (L, 2, RWKV_COLS), f32, 0.0, 0.5),
        "w0": nrm(ks[8], (L, N_DIR, W), 0.5) + 0.5,
        "w_up": nrm(ks[9], (L, N_DIR, DECAY_LORA, W), 0.5 * DECAY_LORA ** -0.5),
        "a0": nrm(ks[10], (L, N_DIR, W), 0.3),
        "a_up": nrm(ks[11], (L, N_DIR, ICLR_LORA, W), 0.5 * ICLR_LORA ** -0.5),
        "k_k": 0.85 + nrm(ks[12], (L, W), 0.05),
        "k_a": 1.0 + nrm(ks[13], (L, W), 0.05),
        "r_k": nrm(ks[14], (L, RWKV_HEADS, HEAD_DIM), 0.1),
        "gn_w": 1.0 + nrm(ks[15], (L, W), 0.02),
        "gn_b": nrm(ks[16], (L, W), 0.01),
        "sink": nrm(ks[17], (L, ATT_HEADS), 1.0),
        "w_br_rwkv": nrm(ks[18], (L, W, D), W ** -0.5),
        "w_br_att": nrm(ks[19], (L, ATT_WIDTH, D), ATT_WIDTH ** -0.5),
        "w_out": nrm(ks[20], (L, D, D), D ** -0.5),
    }


def reference(x, c, w_ada, b_ada, g_pre, g_post, w_in, mu_shift, w0, w_up, a0, a_up,
              k_k, k_a, r_k, gn_w, gn_b, sink, w_br_rwkv, w_br_att, w_out):
    slopes = alibi_slopes(ATT_HEADS)
    cond = jax.nn.silu(c)
    for l in range(DEPTH):
        ada = cond @ w_ada[l] + b_ada[l]
        shift, scale, gate = jnp.split(ada, 3, axis=-1)
        h = rms_norm(x, g_pre[l]) * (1.0 + scale[:, None, :]) + shift[:, None, :]
        p = h @ w_in[l]
        p_rwkv = p[..., :RWKV_COLS]
        p_att = p[..., RWKV_COLS:RWKV_COLS + ATT_COLS]
        gate_logits = p[..., RWKV_COLS + ATT_COLS:]
        y_r = rwkv7_bidir(p_rwkv, mu_shift[l], w0[l], w_up[l], a0[l], a_up[l],
                          k_k[l], k_a[l], r_k[l], gn_w[l], gn_b[l]).astype(x.dtype)
        y_a = windowed_gqa_sink(p_att, slopes, sink[l])
        br_r = y_r @ w_br_rwkv[l]
        br_a = y_a @ w_br_att[l]
        g_r, g_a = jnp.split(jax.nn.sigmoid(gate_logits), 2, axis=-1)
        out = (g_r * br_r + g_a * br_a) @ w_out[l]
        x = x + gate[:, None, :] * rms_norm(out, g_post[l])
    return x
```

```python
import math
import numpy as np
from contextlib import ExitStack
import concourse.bass as bass
import concourse.mybir as mybir
from concourse.bass_utils import run_bass_kernel_spmd

F32 = mybir.dt.float32
BF16 = mybir.dt.bfloat16
I32 = mybir.dt.int32
AF = mybir.ActivationFunctionType
ALU = mybir.AluOpType

ENGS = ["sync", "scalar", "vector", "gpsimd", "tensor"]
S = 2048
NCK = 16
NCH = 44
EXPM05 = math.exp(-0.5)


class Buf:
    def __init__(self, P, t, name, atoms):
        self.P = P
        self.t = t
        self.name = name
        self.own = atoms
        self.kids = {}

    def __getitem__(self, idx):
        return self.t[idx]

    def atoms(self):
        a = list(self.own)
        for k in self.kids.values():
            a += k.own
        return a

    def sub(self, key):
        if key not in self.kids:
            aid = self.P.new_atom()
            st = self.P.state
            st[aid] = {"w": st[self.own[0]]["w"], "r": dict(st[self.own[0]]["r"])}
            self.kids[key] = Buf(self.P, self.t, f"{self.name}.{key}", [aid])
        return self.kids[key]


class Prog:
    def __init__(self, nc, es):
        self.nc = nc
        self.es = es
        self.lists = {e: [] for e in ENGS}
        self.cnt = {e: 0 for e in ENGS}
        self.seen = {e: {} for e in ENGS}
        self.sems = {}
        for e in ENGS:
            self.sems[e] = es.enter_context(nc.semaphore("s_" + e))
        self.dma_cnt = {}
        self.nbuf = 0
        self.state = {}
        self.natom = 0
        self.dbg = {}

    def new_atom(self):
        self.natom += 1
        self.state[self.natom] = {"w": None, "r": {}}
        return self.natom

    def sb(self, shape, dt, name=None):
        self.nbuf += 1
        name = (name or "b") + f"_{self.nbuf}"
        t = self.es.enter_context(self.nc.sbuf_tensor(name, list(shape), dt))
        return Buf(self, t, name, [self.new_atom()])

    def _deps(self, eng, reads, writes):
        waits = {}

        def need(ev):
            if ev is None:
                return
            k, v = ev
            if eng == "tensor" and k == "tensor":
                return
            if self.seen[eng].get(k, 0) >= v:
                return
            waits[k] = max(waits.get(k, 0), v)

        for b in reads:
            for a in b.atoms():
                need(self.state[a]["w"])
        for b in writes:
            for a in b.atoms():
                st = self.state[a]
                need(st["w"])
                for k, v in st["r"].items():
                    need((k, v))
        for k, v in waits.items():
            self.seen[eng][k] = v
        return waits

    def _mark(self, key, c, reads, writes):
        for b in reads:
            for a in b.atoms():
                self.state[a]["r"][key] = c
        for b in writes:
            for a in b.atoms():
                self.state[a]["w"] = (key, c)
                self.state[a]["r"] = {}

    def op(self, eng, fn, reads=(), writes=()):
        waits = self._deps(eng, reads, writes)
        self.cnt[eng] += 1
        self.lists[eng].append((fn, waits, (eng, 1)))
        self._mark(eng, self.cnt[eng], reads, writes)

    def dma(self, eng, fn, reads=(), writes=(), key=None):
        if key not in self.sems:
            self.sems[key] = self.es.enter_context(self.nc.semaphore("s_" + key))
            self.dma_cnt[key] = 0
        waits = self._deps(eng, reads, writes)
        self.dma_cnt[key] += 16
        self.lists[eng].append((fn, waits, (key, 16)))
        self._mark(key, self.dma_cnt[key], reads, writes)

    def final_wait(self, eng, bufs):
        waits = self._deps(eng, bufs, bufs)
        self.lists[eng].append((None, waits, None))

    def emit(self):
        with self.nc.Block() as block:
            def run(engname):
                def f(e):
                    for fn, waits, inc in self.lists[engname]:
                        for k, v in waits.items():
                            e.wait_ge(self.sems[k], v)
                        if fn is None:
                            continue
                        fn(e).then_inc(self.sems[inc[0]], inc[1])
                return f
            block.sync(run("sync"))
            block.scalar(run("scalar"))
            block.vector(run("vector"))
            block.gpsimd(run("gpsimd"))
            block.tensor(run("tensor"))

    def act(self, out, in_, func, r, w, scale=1.0, bias=None, accum=None):
        kw = dict(out=out, in_=in_, func=func, scale=scale)
        if bias is not None:
            kw["bias"] = bias
        if accum is not None:
            kw["accum_out"] = accum
        self.op("scalar", lambda e: e.activation(**kw), r, w)

    def copy(self, eng, out, in_, r, w):
        self.op(eng, lambda e: e.tensor_copy(out=out, in_=in_), r, w)

    def tt(self, eng, out, in0, in1, op, r, w):
        self.op(eng, lambda e: e.tensor_tensor(out=out, in0=in0, in1=in1, op=op), r, w)

    def ts(self, eng, out, in0, s1, op0, r, w, s2=None, op1=None):
        if op1 is None:
            self.op(eng, lambda e: e.tensor_scalar(out=out, in0=in0, scalar1=s1, scalar2=None, op0=op0), r, w)
        else:
            self.op(eng, lambda e: e.tensor_scalar(out=out, in0=in0, scalar1=s1, scalar2=s2, op0=op0, op1=op1), r, w)

    def stt(self, out, in0, scalar, in1, op0, op1, r, w):
        self.op("vector", lambda e: e.scalar_tensor_tensor(out=out, in0=in0, scalar=scalar, in1=in1, op0=op0, op1=op1), r, w)

    def mm(self, out, lhsT, rhs, start, stop, r, w):
        self.op("tensor", lambda e: e.matmul(out, lhsT=lhsT, rhs=rhs, start=start, stop=stop), r, w)

    def tr(self, out, in_, ident, r, w):
        self.op("tensor", lambda e: e.transpose(out=out, in_=in_, identity=ident), r, w)

    def memset(self, eng, ap, val, w):
        self.op(eng, lambda e: e.memset(ap, val), (), w)


def build(nseq=4, dbg=(), stop_after=99):
    nc = bass.Bass("TRN2", target_bir_lowering=False)

    def din(name, shape, dt=F32):
        return nc.dram_tensor(name, list(shape), dt, kind="ExternalInput").ap()

    x = din("x", [nseq, S, 1024])
    cT = din("cT", [128, 8, nseq])
    w_ada = din("w_ada", [24, 128, 1024])
    b_ada = din("b_ada", [128, 24])
    b_gate = din("b_gate", [nseq, 1024])
    gpost_r = din("gpost_r", [nseq, 1024])
    gpre = din("gpre", [128, 8])
    w_inp = din("w_inp", [NCH, 128, 1024])
    w_vatt = din("w_vatt", [128, 1024])
    mu = din("mu", [128, 2, 18])
    w0 = din("w0", [128, 2, 4])
    a0 = din("a0", [128, 2, 4])
    w_up = din("w_up", [128, 512])
    a_up = din("a_up", [128, 512])
    pvec = din("pvec", [128, 6, 4])
    w_brr = din("w_brr", [8, 128, 512])
    w_bra = din("w_bra", [8, 128, 512])
    w_outp = din("w_outp", [128, 8192])
    out = nc.dram_tensor("out", [nseq, S, 1024], F32, kind="ExternalOutput").ap()
    dbg_out = {}

    with ExitStack() as es:
        P = Prog(nc, es)
        psA_t = es.enter_context(nc.psum_tensor("psA", [128, 2048], F32))
        psB_t = es.enter_context(nc.psum_tensor("psB", [128, 2048], F32))
        bank_atoms = [P.new_atom() for _ in range(8)]

        class PsV:
            def __init__(self, b0, n=1):
                self.b0, self.n = b0, n
                self.own = bank_atoms[b0:b0 + n]
                t = psA_t if b0 < 4 else psB_t
                c0 = (b0 % 4) * 512
                self.ap = t[:, c0:c0 + 512 * n]
                self.bf = t[:, c0:c0 + 512 * n].bitcast(BF16)

            def atoms(self):
                return self.own

        bk = [PsV(i) for i in range(8)]
        big = [PsV(0, 4), PsV(4, 4)]

        def dump(name, buf, ap, shape, dt=F32):
            if name not in dbg:
                return
            d = nc.dram_tensor("dbg_" + name, list(shape), dt, kind="ExternalOutput").ap()
            dbg_out[name] = d
            P.dma("gpsimd", lambda e: e.dma_start(out=d, in_=ap), reads=[buf], key="dbg_" + name)
            P.final_wait("gpsimd", [buf])

        TQ = 512
        tA = P.sb([128, TQ], F32, "tA")
        tLW = P.sb([128, TQ], F32, "tLW")
        tP = P.sb([128, 4, 128], F32, "tP")
        tDL = P.sb([128, TQ], F32, "tDL")
        tE = P.sb([128, TQ], F32, "tE")
        tK = P.sb([128, TQ], F32, "tK")
        ones_f = P.sb([128, 128], F32, "ones_f")
        P.memset("gpsimd", ones_f[:], 1.0, [ones_f])
        tmpm = P.sb([128, 128], F32, "tmpm")

        def aff(dst_buf, dst_ap, pattern_step, cm, cmp):
            P.op("gpsimd", lambda e: e.affine_select(out=tmpm[:], in_=ones_f[:], pattern=[[pattern_step, 128]],
                                                     compare_op=cmp, fill=0.0, base=0, channel_multiplier=cm),
                 [ones_f], [tmpm])
            P.copy("gpsimd", dst_ap, tmpm[:], [tmpm], [dst_buf])

        ident = P.sb([128, 128], BF16, "ident")
        aff(ident, ident[:], -1, 1, ALU.is_equal)
        maskT = [P.sb([128, 512], BF16, "maskTf"), P.sb([128, 512], BF16, "maskTb")]
        maskA = [P.sb([128, 128], BF16, "maskAf"), P.sb([128, 128], BF16, "maskAb")]
        SU = (1, -1, ALU.is_gt)
        IU = (1, -1, ALU.is_ge)
        SL = (-1, 1, ALU.is_gt)
        IL = (-1, 1, ALU.is_ge)
        for q, m in enumerate([SU, IU, SU, IU]):
            aff(maskT[0], maskT[0][:, 128 * q:128 * q + 128], *m)
        for q, m in enumerate([SL, IL, SL, IL]):
            aff(maskT[1], maskT[1][:, 128 * q:128 * q + 128], *m)
        aff(maskA[0], maskA[0][:], *SL)
        aff(maskA[1], maskA[1][:], *SU)
        blk32 = P.sb([128, 128], BF16, "blk32")
        off64 = P.sb([128, 256], BF16, "off64")
        off128 = P.sb([128, 128], BF16, "off128")
        for t_ in (blk32, off64, off128):
            P.memset("gpsimd", t_[:], 0.0, [t_])
        for q in range(4):
            P.memset("gpsimd", blk32[32 * q:32 * q + 32, 32 * q:32 * q + 32], 1.0, [blk32])
            q2 = q ^ 1
            for rpt in range(2):
                P.memset("gpsimd", off64[32 * q:32 * q + 32, 128 * rpt + 32 * q2:128 * rpt + 32 * q2 + 32], 1.0, [off64])
        P.memset("gpsimd", off128[0:64, 64:128], 1.0, [off128])
        P.memset("gpsimd", off128[64:128, 0:64], 1.0, [off128])
        mM = [P.sb([128, 128], BF16, "mMf"), P.sb([128, 128], BF16, "mMb")]
        for d_ in range(2):
            P.tt("gpsimd", mM[d_][:], maskA[d_][:], blk32[:], ALU.mult, [maskA[d_], blk32], [mM[d_]])
        ones_blk = P.sb([128, 128], F32, "ones_blk")
        P.memset("gpsimd", ones_blk[:], 1.0, [ones_blk])
        P.memset("gpsimd", ones_blk[0:64, 64:128], 0.0, [ones_blk])
        P.memset("gpsimd", ones_blk[64:128, 0:64], 0.0, [ones_blk])
        onesLR = P.sb([128, 2, 128], BF16, "onesLR")
        P.memset("gpsimd", onesLR[:], 0.0, [onesLR])
        P.memset("gpsimd", onesLR[:, 0, 0:64], 1.0, [onesLR])
        P.memset("gpsimd", onesLR[:, 1, 64:128], 1.0, [onesLR])
        cmask = P.sb([128, 4, 128], BF16, "cmask")
        P.memset("gpsimd", cmask[:], 1.0, [cmask])
        P.memset("gpsimd", cmask[:, :, 0:1], 0.0, [cmask])
        sel = P.sb([nseq, nseq, 128], F32, "sel")
        P.memset("gpsimd", sel[:], 1.0, [sel])
        P.op("gpsimd", lambda e: e.affine_select(out=sel[:], in_=sel[:], pattern=[[-1, nseq], [0, 128]],
                                                 compare_op=ALU.is_equal, fill=0.0, base=0, channel_multiplier=1),
             [sel], [sel])
        ebias = P.sb([128, 8, 3, 128], BF16, "ebias")
        di_ap = tLW[:, 0:384].bitcast(I32).rearrange("p (s c) -> p s c", s=3)
        df_ap = tE[:, 0:384].rearrange("p (s c) -> p s c", s=3)
        et_ap = tK[:, 0:384].rearrange("p (s c) -> p s c", s=3)
        vm_ap = tA[:, 0:256].rearrange("p (s c) -> p s c", s=2)
        P.op("gpsimd", lambda e: e.iota(di_ap[:, 0, :], pattern=[[1, 128]], base=128, channel_multiplier=-1), (), [tLW])
        P.op("gpsimd", lambda e: e.iota(di_ap[:, 1, :], pattern=[[1, 128]], base=0, channel_multiplier=-1), (), [tLW])
        P.op("gpsimd", lambda e: e.iota(di_ap[:, 2, :], pattern=[[-1, 128]], base=128, channel_multiplier=1), (), [tLW])
        P.copy("vector", df_ap, di_ap, [tLW], [tE])
        P.act(df_ap[:, 1, :], df_ap[:, 1, :], AF.Abs, [tE], [tE])
        aff(tA, vm_ap[:, 0, :], *IL)
        aff(tA, vm_ap[:, 1, :], *IU)
        for h in range(8):
            slope = 2.0 ** (-(h + 1))
            P.act(et_ap, df_ap, AF.Exp, [tE], [tK], scale=-slope)
            P.tt("vector", ebias[:, h, 0, :], et_ap[:, 0, :], vm_ap[:, 0, :], ALU.mult, [tK, tA], [ebias])
            P.copy("vector", ebias[:, h, 1, :], et_ap[:, 1, :], [tK], [ebias])
            P.tt("vector", ebias[:, h, 2, :], et_ap[:, 2, :], vm_ap[:, 1, :], ALU.mult, [tK, tA], [ebias])

        def load_small(src, shape, name):
            b = P.sb(shape, F32, name)
            P.dma("sync", lambda e: e.dma_start(out=b[:], in_=src), writes=[b], key="ld_" + name)
            return b

        mu_t = load_small(mu, [128, 2, 18], "mu")
        w0_t = load_small(w0, [128, 2, 4], "w0")
        a0_t = load_small(a0, [128, 2, 4], "a0")
        pv = load_small(pvec, [128, 6, 4], "pv")
        gpre_t = load_small(gpre, [128, 8], "gpre")
        bada_t = load_small(b_ada, [128, 24], "bada")
        cT_t = load_small(cT, [128, 8, nseq], "cT")
        c0_t = P.sb([128, 18], F32, "c0")
        P.tt("vector", c0_t[:], mu_t[:, 0, :], mu_t[:, 1, :], ALU.add, [mu_t], [c0_t])
        P.ts("vector", c0_t[:], c0_t[:], -1.0, ALU.mult, [c0_t], [c0_t], s2=1.0, op1=ALU.add)
        omka = P.sb([128, 4], F32, "omka")
        P.ts("vector", omka[:], pv[:, 1, :], -1.0, ALU.mult, [pv], [omka], s2=1.0, op1=ALU.add)
        esink = P.sb([128, 4], F32, "esink")
        P.act(esink[:], pv[:, 5, :], AF.Exp, [pv], [esink])
        eps_t = P.sb([128, 1], F32, "eps")
        P.memset("vector", eps_t[:], 1e-6, [eps_t])
        gneps_t = P.sb([128, 1], F32, "gneps")
        P.memset("vector", gneps_t[:], 64e-5, [gneps_t])

        hT = P.sb([128, 8, S], BF16, "hT")
        hTf = hT[:].rearrange("p k s -> p (k s)").bitcast(F32)
        NSS = 8
        NW = 3
        wbf = [P.sb([128, 8, 128], BF16, f"wbf{i}") for i in range(NW)]
        wctr = [0]

        def load_cast(dst_buf, dst_ap, src_ap, ncols):
            c = wctr[0]
            wctr[0] += 1
            sb_ = hT.sub(("st", c % NSS))
            sap = hTf[:, 1024 * (c % NSS):1024 * (c % NSS) + ncols]
            P.dma("sync", lambda e: e.dma_start(out=sap, in_=src_ap), writes=[sb_], key=f"wst{c % NSS}")
            if c % 2:
                P.act(dst_ap, sap, AF.Copy, [sb_], [dst_buf])
            else:
                P.copy("vector", dst_ap, sap, [sb_], [dst_buf])

        wsc_in = nc.dram_tensor("wsc_in", [NCH, 128, 1024], BF16).ap()
        wsc_br = nc.dram_tensor("wsc_br", [16, 128, 512], BF16).ap()
        WSC = Buf(P, None, "wsc", [P.new_atom()])
        wbc = [0]
        for ci in range(NCH + 16):
            wb = wbf[wbc[0] % NW]
            wbc[0] += 1
            if ci < NCH:
                srcw, dstw, ncw = w_inp[ci], wsc_in[ci], 1024
            elif ci < NCH + 8:
                srcw, dstw, ncw = w_brr[ci - NCH], wsc_br[ci - NCH], 512
            else:
                srcw, dstw, ncw = w_bra[ci - NCH - 8], wsc_br[ci - NCH], 512
            wflat = wb[:].rearrange("p k c -> p (k c)")[:, 0:ncw]
            load_cast(wb, wflat, srcw, ncw)
            P.dma("sync", lambda e, dstw=dstw, wflat=wflat: e.dma_start(out=dstw, in_=wflat), reads=[wb], writes=[WSC], key="wsc_w")

        def load_w(src_ap, ncols=1024):
            i = wbc[0] % NW
            wbc[0] += 1
            wb = wbf[i]
            P.dma("sync", lambda e: e.dma_start(out=wb[:].rearrange("p k c -> p (k c)")[:, 0:ncols], in_=src_ap), reads=[WSC], writes=[wb], key=f"wbf{i}")
            return wb

        wout_b = P.sb([128, 8, 1024], BF16, "wout_b")
        for k in range(8):
            load_cast(wout_b, wout_b[:, k, :], w_outp[:, 1024 * k:1024 * k + 1024], 1024)
        wv_b = P.sb([128, 8, 128], BF16, "wv_b")
        load_cast(wv_b, wv_b[:].rearrange("p k c -> p (k c)"), w_vatt, 1024)
        wup_b = P.sb([128, 512], BF16, "wupb")
        aup_b = P.sb([128, 512], BF16, "aupb")
        load_cast(wup_b, wup_b[:], w_up, 512)
        load_cast(aup_b, aup_b[:], a_up, 512)

        Kf = P.sb([128, S], F32, "Kf")
        Yf = P.sb([128, S], F32, "Yf")
        bgate_t = Kf
        gpostr_t = Kf
        P.dma("sync", lambda e: e.dma_start(out=Kf[0:nseq, 0:1024], in_=b_gate), writes=[Kf], key="ld_bgate")
        P.dma("sync", lambda e: e.dma_start(out=Kf[0:nseq, 1024:2048], in_=gpost_r), writes=[Kf], key="ld_bgate")
        cond = P.sb([128, 8, nseq], F32, "cond")
        P.act(cond[:], cT_t[:], AF.Silu, [cT_t], [cond])
        adaT = P.sb([128, 24, nseq], F32, "adaT")
        garow = P.sb([nseq, 1024], F32, "garow")
        for jj in range(24):
            i_ = jj % 2
            wa = Yf.sub(i_)
            wa_ap = Yf[:, 1024 * i_:1024 * i_ + 1024]
            P.dma("sync", lambda e, wa_ap=wa_ap, jj=jj: e.dma_start(out=wa_ap, in_=w_ada[jj]), writes=[wa], key=f"wada{i_}")
            pb = bk[jj % 2]
            for k in range(8):
                P.mm(pb.ap[:, 0:nseq], wa_ap[:, 128 * k:128 * k + 128], cond[:, k, :], k == 0, k == 7, [wa, cond], [pb])
            P.act(adaT[:, jj, :], pb.ap[:, 0:nseq], AF.Identity, [pb, bada_t], [adaT], bias=bada_t[:, jj:jj + 1])
            if jj >= 16:
                pr = bk[2 + jj % 2]
                for k in range(8):
                    P.mm(pr.ap[0:nseq, 0:128], cond[:, k, :], wa_ap[:, 128 * k:128 * k + 128], k == 0, k == 7, [wa, cond], [pr])
                c0 = 128 * (jj - 16)
                P.tt("vector", garow[:, c0:c0 + 128], pr.ap[0:nseq, 0:128], bgate_t[0:nseq, c0:c0 + 128], ALU.add, [pr, bgate_t], [garow])
        P.tt("vector", garow[:], garow[:], gpostr_t[0:nseq, 1024:2048], ALU.mult, [garow, gpostr_t], [garow])
        preA = P.sb([128, 8, nseq], F32, "preA")
        P.ts("vector", preA[:], adaT[:, 8:16, :], 1.0, ALU.add, [adaT], [preA])
        for b in range(nseq):
            P.tt("vector", preA[:, :, b], preA[:, :, b], gpre_t[:], ALU.mult, [preA, gpre_t], [preA])
        dump("adaT", adaT, adaT[:], [128, 24, nseq])
        dump("garow", garow, garow[:], [nseq, 1024])

        TW = P.sb([128, S], BF16, "TW")
        AL = P.sb([128, S], BF16, "AL")
        YR = P.sb([128, 4, S], BF16, "YR")
        xn = P.sb([128, 1024], BF16, "xn")
        st4 = P.sb([128, 8], F32, "st4")
        GPb = P.sb([128, 1024], BF16, "GPb")
        Rb = P.sb([128, S], BF16, "Rb")
        Vb = P.sb([128, S], BF16, "Vb")
        KKf = P.sb([128, S], BF16, "KKb")
        SG = P.sb([128, S], BF16, "SG")
        xt = [Rb, Vb]
        xap = [Rb[:].bitcast(F32), Vb[:].bitcast(F32)]
        junk = xn
        BON = P.sb([128, S], BF16, "BON")
        VTM = P.sb([128, NCK, 128], BF16, "VTM")
        AR = P.sb([128, NCK, 256], BF16, "AR")
        KT = P.sb([128, S], BF16, "KT")
        BT = P.sb([128, S], BF16, "BT")
        KTM = P.sb([128, 4, 128], BF16, "KTM")
        BTM = P.sb([128, 4, 128], BF16, "BTM")
        Ffac = P.sb([128, NCK], F32, "Ffac")
        P63 = P.sb([128, NCK], F32, "P63")
        Qd = P.sb([128, NCK], F32, "Qd")
        HC = 4
        AT4 = P.sb([128, 2 * HC, 512], BF16, "AT4")
        TT = P.sb([128, 2 * HC, 128], BF16, "TT")
        STG = P.sb([128, 8, 2, 384], BF16, "STG")
        STGv = STG.t
        stgs = [[Buf(P, STGv[:, i, j, :], f"stg{i}_{j}", [P.new_atom()]) for j in range(2)] for i in range(8)]
        AFt = [P.sb([128, 128], BF16, f"AF{i}") for i in range(8)]
        T64t = P.sb([128, 8, 256], BF16, "T64t")
        T64v = T64t.t
        T64 = [Buf(P, T64v[:, i, :], f"T64_{i}", [P.new_atom()]) for i in range(8)]
        HS = [P.sb([128, 64], BF16, f"HS{i}") for i in range(2)]
        Wsb = P.sb([128, 128], BF16, "Wsb")
        Usb = P.sb([128, 128], BF16, "Usb")

        def proj_full(wb, pbig):
            for i in range(4):
                for k in range(8):
                    P.mm(pbig.ap[:, 512 * i:512 * i + 512], wb[:, k, :], hT[:, k, 512 * i:512 * i + 512], k == 0, k == 7, [wb, hT], [pbig])

        def shift_evac(pbig, ci, dst, dst_ap_fn):
            P.act(dst_ap_fn(0, S), pbig.ap[:, 0:S], AF.Copy, [pbig, c0_t], [dst], scale=c0_t[:, ci:ci + 1])
            P.stt(dst_ap_fn(1, S), pbig.ap[:, 0:S - 1], mu_t[:, 0, ci:ci + 1], dst_ap_fn(1, S), ALU.mult, ALU.add, [pbig, mu_t, dst], [dst])
            P.stt(dst_ap_fn(0, S - 1), pbig.ap[:, 1:S], mu_t[:, 1, ci:ci + 1], dst_ap_fn(0, S - 1), ALU.mult, ALU.add, [pbig, mu_t, dst], [dst])

        pcnt = [0]

        def next_big():
            pcnt[0] += 1
            return big[pcnt[0] % 2]

        for b in range(nseq):
            for tb in range(16):
                xb_ = xt[tb % 2]
                xa_ = xap[tb % 2]
                P.dma("sync", lambda e, xa_=xa_, tb=tb, b=b: e.dma_start(out=xa_, in_=x[b, 128 * tb:128 * tb + 128, :]), writes=[xb_], key=f"xt{tb % 2}")
                P.act(junk[:], xa_, AF.Square, [xb_], [junk, st4], accum=st4[:, 0:1])
                P.act(st4[:, 1:2], st4[:, 0:1], AF.Ln, [st4, eps_t], [st4], scale=1.0 / 1024, bias=eps_t[:, 0:1])
                P.act(st4[:, 2:3], st4[:, 1:2], AF.Exp, [st4], [st4], scale=-0.5)
                P.ts("vector", xn[:], xa_, st4[:, 2:3], ALU.mult, [xb_, st4], [xn])
                pb = bk[tb % 2]
                for k in range(8):
                    P.tr(pb.bf[:, 128 * k:128 * k + 128], xn[:, 128 * k:128 * k + 128], ident[:], [xn, ident], [pb])
                for k in range(8):
                    P.act(hT[:, k, 128 * tb:128 * tb + 128], pb.bf[:, 128 * k:128 * k + 128], AF.Identity, [pb, preA, adaT], [hT],
                          scale=preA[:, k, b:b + 1], bias=adaT[:, k, b:b + 1])
            dump(f"hT{b}", hT, hT[:], [128, 8, S], BF16)
            if stop_after <= 1:
                continue
            for ci, (dst, fn) in enumerate([(TW, AF.Tanh), (AL, AF.Copy)]):
                wb = load_w(wsc_in[ci])
                pbig = next_big()
                proj_full(wb, pbig)
                tS = Kf
                shift_evac(pbig, ci, tS, lambda a, c: tS[:, a:c])
                P.act(dst[:], tS[:], fn, [tS], [dst])
            dump(f"TW{b}", TW, TW[:], [128, S], BF16)
            dump(f"AL{b}", AL, AL[:], [128, S], BF16)
            if stop_after <= 2:
                continue
            for hp in range(4):
                for j, name in enumerate("rkvg"):
                    ci = 2 + 4 * hp + j
                    wb = load_w(wsc_in[ci])
                    pbig = next_big()
                    proj_full(wb, pbig)
                    if name == "k":
                        shift_evac(pbig, ci, Kf, lambda a, c: Kf[:, a:c])
                    else:
                        tS = Yf
                        shift_evac(pbig, ci, tS, lambda a, c: tS[:, a:c])
                        if name == "r":
                            P.act(Rb[:], tS[:], AF.Copy, [tS], [Rb])
                        elif name == "v":
                            P.act(Vb[:], tS[:], AF.Copy, [tS], [Vb])
                        else:
                            P.act(SG[:], tS[:], AF.Silu, [tS], [SG])
                P.ts("vector", Yf[:], Kf[:], pv[:, 0, hp:hp + 1], ALU.mult, [Kf, pv], [Yf])
                for i in range(4):
                    sl = slice(512 * i, 512 * i + 512)
                    P.act(tE[:], Yf[:, sl], AF.Square, [Yf], [tE])
                    pb = bk[i % 2]
                    P.mm(pb.ap[:], ones_blk[:], tE[:], True, True, [ones_blk, tE], [pb])
                    P.ts("vector", tK[:], pb.ap[:], 1e-24, ALU.max, [pb], [tK])
                    P.act(tK[:], tK[:], AF.Ln, [tK], [tK])
                    P.act(tK[:], tK[:], AF.Exp, [tK], [tK], scale=-0.5)
                    P.tt("vector", KKf[:, sl], Yf[:, sl], tK[:], ALU.mult, [Yf, tK], [KKf])
                dump(f"KK{b}_{hp}", KKf, KKf[:], [128, S], BF16)
                dump(f"R{b}_{hp}", Rb, Rb[:], [128, S], BF16)
                for g4 in range(4):
                    pb = bk[2 + g4 % 2]
                    for q in range(4):
                        n = 4 * g4 + q
                        P.tr(pb.bf[:, 128 * q:128 * q + 128], Vb[:, 128 * n:128 * n + 128], ident[:], [Vb, ident], [pb])
                    P.copy("vector", VTM[:, 4 * g4:4 * g4 + 4, :], pb.bf[:, 0:512].rearrange("p (q c) -> p q c", q=4), [pb], [VTM])
                for d in range(2):
                    rows = slice(64 * d, 64 * d + 64)
                    for i in range(4):
                        sl = slice(512 * i, 512 * i + 512)
                        pa = bk[0 + i % 2]
                        pw = bk[2 + i % 2]
                        P.mm(pa.ap[:], aup_b[rows, 128 * hp:128 * hp + 128], AL[rows, sl], True, True, [aup_b, AL], [pa])
                        P.mm(pw.ap[:], wup_b[rows, 128 * hp:128 * hp + 128], TW[rows, sl], True, True, [wup_b, TW], [pw])
                        P.act(tA[:], pa.ap[:], AF.Sigmoid, [pa, a0_t], [tA], bias=a0_t[:, d, hp:hp + 1])
                        P.act(tLW[:], pw.ap[:], AF.Sigmoid, [pw, w0_t], [tLW], bias=w0_t[:, d, hp:hp + 1])
                        P.ts("vector", tLW[:], tLW[:], -EXPM05, ALU.mult, [tLW], [tLW])
                        P.op("vector", lambda e, i=i: e.tensor_tensor_scan(out=tP[:].rearrange("p n c -> p (n c)"),
                                                                          data0=cmask[:].rearrange("p n c -> p (n c)"),
                                                                          data1=tLW[:], initial=0.0, op0=ALU.mult, op1=ALU.add),
                             [cmask, tLW], [tP])
                        P.copy("vector", P63[:, 4 * i:4 * i + 4], tP[:, :, 63], [tP], [P63])
                        P.tt("vector", Qd[:, 4 * i:4 * i + 4], tP[:, :, 127], tP[:, :, 63], ALU.subtract, [tP], [Qd])
                        P.tt("vector", tP[:], tP[:], P63[:, 4 * i:4 * i + 4].unsqueeze(2).to_broadcast([128, 4, 128]), ALU.subtract, [tP, P63], [tP])
                        tPf = tP[:].rearrange("p n c -> p (n c)")
                        P.tt("vector", tDL[:], tPf, tLW[:], ALU.subtract, [tP, tLW], [tDL])
                        ARv = AR[:, 4 * i:4 * i + 4, :]
                        P.ts("vector", tK[:], tA[:], pv[:, 1, hp:hp + 1], ALU.mult, [tA, pv, omka], [tK], s2=omka[:, hp:hp + 1], op1=ALU.add)
                        P.tt("vector", tK[:], tK[:], Kf[:, sl], ALU.mult, [tK, Kf], [tK])
                        P.stt(tE[:], tK[:], pv[:, 2, hp:hp + 1], Rb[:, sl], ALU.mult, ALU.mult, [tK, pv, Rb], [tE])
                        pbn = bk[4 + i % 2]
                        P.mm(pbn.ap[:], ones_blk[:], tE[:], True, True, [ones_blk, tE], [pbn])
                        if d == 0:
                            P.copy("vector", BON[:, sl], pbn.ap[:], [pbn], [BON])
                        else:
                            P.tt("vector", BON[:, sl], BON[:, sl], pbn.ap[:], ALU.add, [pbn, BON], [BON])
                        P.tt("gpsimd", tA[:], tA[:], KKf[:, sl], ALU.mult, [tA, KKf], [tA])
                        if d == 0:
                            s_e1, in_e1, s_e2, in_e2, s_e1x, in_e1x = 1.0, tPf, -1.0, tPf, 1.0, tDL[:]
                            r_e1, r_e2, r_e1x = tP, tP, tDL
                        else:
                            s_e1, in_e1, s_e2, in_e2, s_e1x, in_e1x = -1.0, tDL[:], 1.0, tDL[:], -1.0, tPf
                            r_e1, r_e2, r_e1x = tDL, tDL, tP
                        P.act(tE[:], in_e1, AF.Exp, [r_e1], [tE], scale=s_e1)
                        P.tt("vector", ARv[:, :, 128:256], tE[:].rearrange("p (n c) -> p n c", n=4), Rb[:, sl].rearrange("p (n c) -> p n c", n=4), ALU.mult, [tE, Rb], [AR])
                        P.act(tE[:], in_e1x, AF.Exp, [r_e1x], [tE], scale=s_e1x)
                        P.stt(ARv[:, :, 0:128], tE[:].rearrange("p (n c) -> p n c", n=4), -1.0, KKf[:, sl].rearrange("p (n c) -> p n c", n=4), ALU.mult, ALU.mult, [tE, KKf], [AR])
                        P.act(tE[:], in_e2, AF.Exp, [r_e2], [tE], scale=s_e2)
                        P.tt("vector", KT[:, sl], tE[:], tK[:], ALU.mult, [tE, tK], [KT])
                        P.tt("gpsimd", BT[:, sl], tE[:], tA[:], ALU.mult, [tE, tA], [BT])
                    P.tt("vector", Ffac[:, 0:NCK - 1], Qd[:, 0:NCK - 1], P63[:, 1:NCK], ALU.add, [Qd, P63], [Ffac])
                    P.act(Ffac[:, 0:NCK - 1], Ffac[:, 0:NCK - 1], AF.Exp, [Ffac], [Ffac])
                    if f"AR{b}_{hp}_{d}" in dbg:
                        dump(f"AR{b}_{hp}_{d}", AR, AR[:], [128, NCK, 256], BF16)
                        dump(f"KT{b}_{hp}_{d}", KT, KT[:], [128, S], BF16)
                        dump(f"BT{b}_{hp}_{d}", BT, BT[:], [128, S], BF16)
                        dump(f"F{b}_{hp}_{d}", Ffac, Ffac[:], [128, NCK])
                    if stop_after <= 3:
                        continue
                    order = list(range(NCK)) if d == 0 else list(range(NCK - 1, -1, -1))
                    hsi = 0
                    for half in range(NCK // HC):
                        chunks = order[HC * half:HC * half + HC]
                        for qi, (src_, dstm) in enumerate(((KT, KTM), (BT, BTM))):
                            pb = bk[2 + qi]
                            for li, n in enumerate(chunks):
                                P.tr(pb.bf[:, 128 * li:128 * li + 128], src_[:, 128 * n:128 * n + 128], ident[:], [src_, ident], [pb])
                            P.copy("vector", dstm[:], pb.bf[:, 0:512].rearrange("p (q c) -> p q c", q=4), [pb], [dstm])
                        for g0 in range(0, HC, 4):
                            insts = [(li, h) for li in range(g0, g0 + 4) for h in range(2)]
                            for ii, (li, h) in enumerate(insts):
                                n = chunks[li]
                                hr = slice(64 * h, 64 * h + 64)
                                slot = 2 * li + h
                                cs = slice(128 * n, 128 * n + 128)
                                pm = bk[ii]
                                P.mm(pm.ap[:, 0:256], BT[hr, cs], AR[hr, n, :], True, True, [BT, AR], [pm])
                                P.mm(pm.ap[:, 256:512], KT[hr, cs], AR[hr, n, :], True, True, [KT, AR], [pm])
                                pa2 = bk[(ii + 4) % 8]
                                P.mm(pa2.ap[:, 0:128], AR[hr, n, 0:128], BT[hr, cs], True, True, [AR, BT], [pa2])
                                a4 = AT4.sub(slot)
                                P.tt("vector", AT4[:, slot, :], pm.ap[:], maskT[d][:], ALU.mult, [pm, maskT[d]], [a4])
                                s0 = stgs[ii][0]
                                P.tt("vector", AFt[ii][:], pa2.ap[:, 0:128], maskA[d][:], ALU.mult, [pa2, maskA[d]], [AFt[ii]])
                                P.tt("vector", s0[:, 256:384], pa2.ap[:, 0:128], mM[d][:], ALU.mult, [pa2, mM[d]], [s0])
                                P.tt("gpsimd", s0[:, 128:256], AT4[:, slot, 0:128], blk32[:], ALU.mult, [a4, blk32], [s0])
                                P.copy("gpsimd", s0[:, 0:128], ident[:], [ident], [s0])
                            def grp(g):
                                ids = list(range(4 * g, 4 * g + 4))
                                pst = psA_t if g == 0 else psB_t
                                psg = pst[:, 0:2048].rearrange("p (b c) -> p b c", b=4)
                                psg_bf = pst[:, 0:2048].bitcast(BF16).rearrange("p (b c) -> p b c", b=4)
                                return ids, psg, psg_bf, [bk[i] for i in ids], slice(4 * g, 4 * g + 4)
                            for lvl in range(1, 6):
                                cp, np_ = (lvl - 1) % 2, lvl % 2
                                for g in range(2):
                                    ids, psg, psg_bf, banks, gs = grp(g)
                                    for ii in ids:
                                        cur = stgs[ii][cp]
                                        pl = bk[ii]
                                        if lvl <= 4:
                                            P.mm(pl.ap[:, 0:256], cur[:, 256:384], cur[:, 0:256], True, True, [cur], [pl])
                                            P.mm(pl.ap[:, 256:384], cur[:, 128:256], cur[:, 256:384], True, True, [cur], [pl])
                                        else:
                                            P.mm(pl.ap[:, 0:128], cur[:, 256:384], cur[:, 0:128], True, True, [cur], [pl])
                                    curs = [stgs[i][cp] for i in ids]
                                    nxts = [stgs[i][np_] for i in ids]
                                    if lvl <= 4:
                                        P.act(STGv[:, gs, np_, 128:384], psg[:, :, 128:384], AF.Copy, banks, nxts)
                                    P.tt("vector", STGv[:, gs, np_, 0:128], psg[:, :, 0:128], STGv[:, gs, cp, 0:128], ALU.add, banks + curs, nxts)
                            for g in range(2):
                                ids, psg, psg_bf, banks, gs = grp(g)
                                fins = [stgs[i][1] for i in ids]
                                for ii in ids:
                                    P.tr(bk[ii].bf[:, 0:128], stgs[ii][1][:, 0:128], ident[:], [stgs[ii][1], ident], [bk[ii]])
                                P.act(STGv[:, gs, 1, 128:256], psg_bf[:, :, 0:128], AF.Copy, banks, fins)
                            for g in range(2):
                                ids, psg, psg_bf, banks, gs = grp(g)
                                for ii in ids:
                                    fin = stgs[ii][1]
                                    pl = bk[ii]
                                    P.mm(pl.ap[:, 0:128], AT4[:, ii, 0:128], fin[:, 128:256], True, True, [AT4.sub(ii), fin], [pl])
                                    P.mm(pl.ap[:, 128:256], AFt[ii][:], fin[:, 0:128], True, True, [AFt[ii], fin], [pl])
                                P.tt("vector", STGv[:, gs, 0, 0:256], psg[:, :, 0:256], off64[:].unsqueeze(1).to_broadcast([128, 4, 256]), ALU.mult,
                                     banks + [off64], [stgs[i][0] for i in ids])
                            for g in range(2):
                                ids, psg, psg_bf, banks, gs = grp(g)
                                for ii in ids:
                                    fin = stgs[ii][1]
                                    pl = bk[ii]
                                    P.mm(pl.ap[:, 256:384], fin[:, 128:256], stgs[ii][0][:, 128:256], True, True, [fin, stgs[ii][0]], [pl])
                                    P.mm(pl.ap[:, 384:512], fin[:, 0:128], stgs[ii][0][:, 0:128], True, True, [fin, stgs[ii][0]], [pl])
                                P.tt("vector", T64v[:, gs, :], psg[:, :, 256:512], STGv[:, gs, 1, 0:256], ALU.add,
                                     banks + [stgs[i][1] for i in ids], [T64[i] for i in ids])
                            for g in range(2):
                                ids, psg, psg_bf, banks, gs = grp(g)
                                for ii in ids:
                                    P.mm(bk[ii].ap[:, 0:128], AFt[ii][:], T64[ii][:, 0:128], True, True, [AFt[ii], T64[ii]], [bk[ii]])
                                P.tt("vector", STGv[:, gs, 0, 0:128], psg[:, :, 0:128], off128[:].unsqueeze(1).to_broadcast([128, 4, 128]), ALU.mult,
                                     banks + [off128], [stgs[i][0] for i in ids])
                            for g in range(2):
                                ids, psg, psg_bf, banks, gs = grp(g)
                                for ii in ids:
                                    P.mm(bk[ii].ap[:, 128:256], T64[ii][:, 128:256], stgs[ii][0][:, 0:128], True, True, [T64[ii], stgs[ii][0]], [bk[ii]])
                                P.tt("vector", TT[:, gs, :], psg[:, :, 128:256], T64v[:, gs, 0:128], ALU.add,
                                     banks + [T64[i] for i in ids], [TT.sub(i) for i in ids])
                        if f"TT{b}_{hp}_{d}_{half}" in dbg:
                            dump(f"TT{b}_{hp}_{d}_{half}", TT, TT[:], [128, 2 * HC, 128], BF16)
                            dump(f"AT4{b}_{hp}_{d}_{half}", AT4, AT4[:], [128, 2 * HC, 512], BF16)
                        for li, n in enumerate(chunks):
                            gi = HC * half + li
                            cs = slice(128 * n, 128 * n + 128)
                            first = gi == 0
                            Hc = HS[hsi % 2]
                            Hn = HS[(hsi + 1) % 2]
                            hsi += 1
                            pw_ = bk[0]
                            pu_ = bk[1]
                            py_ = bk[2]
                            pg_ = bk[3]
                            for h in range(2):
                                hr = slice(64 * h, 64 * h + 64)
                                hc = slice(64 * h, 64 * h + 64)
                                slot = 2 * li + h
                                a4 = AT4.sub(slot)
                                if not first:
                                    P.mm(pw_.ap[:, hc], AR[hr, n, 0:128], Hc[hr, :], True, False, [AR, Hc], [pw_])
                                P.mm(pw_.ap[:, hc], AT4[:, slot, 256:384], VTM[:, n, hc], first, True, [a4, VTM], [pw_])
                            last = gi == NCK - 1
                            for h in range(2):
                                hr = slice(64 * h, 64 * h + 64)
                                hc = slice(64 * h, 64 * h + 64)
                                slot = 2 * li + h
                                a4 = AT4.sub(slot)
                                if not last:
                                    if not first:
                                        P.mm(pg_.ap[hr, 0:64], ident[hr, hr], Hc[hr, :], True, False, [ident, Hc], [pg_])
                                    P.mm(pg_.ap[hr, 0:64], KTM[:, li, hc], VTM[:, n, hc], first, False, [KTM, VTM], [pg_])
                                if not first:
                                    P.mm(py_.ap[hr, 0:128], Hc[hr, :], AR[hr, n, 128:256], True, False, [Hc, AR], [py_])
                                P.mm(py_.ap[hr, 0:128], VTM[:, n, hc], AT4[:, slot, 384:512], first, False, [VTM, a4], [py_])
                            P.copy("vector", Wsb[:], pw_.ap[:, 0:128], [pw_], [Wsb])
                            for h in range(2):
                                hc = slice(64 * h, 64 * h + 64)
                                slot = 2 * li + h
                                P.mm(pu_.ap[:, hc], TT[:, slot, :], Wsb[:, hc], True, True, [TT.sub(slot), Wsb], [pu_])
                            P.act(Usb[:], pu_.ap[:, 0:128], AF.Copy, [pu_], [Usb])
                            if not last:
                                for h in range(2):
                                    hr = slice(64 * h, 64 * h + 64)
                                    hc = slice(64 * h, 64 * h + 64)
                                    P.mm(pg_.ap[hr, 0:64], BTM[:, li, hc], Usb[:, hc], False, True, [BTM, Usb], [pg_])
                                fi = n if d == 0 else n - 1
                                P.act(Hn[:], pg_.ap[:, 0:64], AF.Copy, [pg_, Ffac], [Hn], scale=Ffac[:, fi:fi + 1])
                            for h in range(2):
                                hr = slice(64 * h, 64 * h + 64)
                                hc = slice(64 * h, 64 * h + 64)
                                slot = 2 * li + h
                                P.mm(py_.ap[hr, 0:128], Usb[:, hc], AT4[:, slot, 128:256], False, True, [Usb, AT4.sub(slot)], [py_])
                            if d == 0:
                                P.copy("vector", Yf[:, cs], py_.ap[:, 0:128], [py_], [Yf])
                            else:
                                P.tt("vector", Yf[:, cs], Yf[:, cs], py_.ap[:, 0:128], ALU.add, [py_, Yf], [Yf])
                dump(f"Y{b}_{hp}", Yf, Yf[:], [128, S])
                if stop_after <= 4:
                    continue
                for i in range(4):
                    sl = slice(512 * i, 512 * i + 512)
                    pm_ = bk[4 + i % 2]
                    pv_ = bk[6 + i % 2]
                    P.mm(pm_.ap[:], ones_blk[:], Yf[:, sl], True, True, [ones_blk, Yf], [pm_])
                    P.stt(tE[:], pm_.ap[:], -1.0 / 64, Yf[:, sl], ALU.mult, ALU.add, [pm_, Yf], [tE])
                    P.act(tK[:], tE[:], AF.Square, [tE], [tK])
                    P.mm(pv_.ap[:], ones_blk[:], tK[:], True, True, [ones_blk, tK], [pv_])
                    P.act(tK[:], pv_.ap[:], AF.Ln, [pv_, gneps_t], [tK], scale=1.0 / 64, bias=gneps_t[:, 0:1])
                    P.act(tK[:], tK[:], AF.Exp, [tK], [tK], scale=-0.5)
                    P.tt("vector", tE[:], tE[:], tK[:], ALU.mult, [tE, tK], [tE])
                    P.ts("vector", tE[:], tE[:], pv[:, 3, hp:hp + 1], ALU.mult, [tE, pv], [tE], s2=pv[:, 4, hp:hp + 1], op1=ALU.add)
                    P.tt("gpsimd", tK[:], BON[:, sl], Vb[:, sl], ALU.mult, [BON, Vb], [tK])
                    P.tt("vector", tE[:], tE[:], tK[:], ALU.add, [tE, tK], [tE])
                    P.tt("vector", YR[:, hp, sl], tE[:], SG[:, sl], ALU.mult, [tE, SG], [YR])
            dump(f"YR{b}", YR, YR[:], [128, 4, S], BF16)
            if stop_after <= 5:
                continue
            QTb, QT = Kf, Kf[:].bitcast(BF16)[:, 0:3072].rearrange("p (j c) -> p j c", j=4)
            V2b, V2 = Yf, Yf[:].bitcast(BF16)[:, 0:3072].rearrange("p (w k l c) -> p w k l c", w=6, k=2, l=2)
            KAb, KA = AR, AR[:].rearrange("p n c -> p (n c)")[:, 0:1536].rearrange("p (k c) -> p k c", k=2)
            PTb, PT = AR, AR[:].rearrange("p n c -> p (n c)")[:, 1536:2304].rearrange("p (h c) -> p h c", h=2)
            GAb, GA = KT, KT[:].rearrange("p (j c) -> p j c", j=4)
            YAb, YA = BT, BT[:].rearrange("p (j c) -> p j c", j=4)
            MGb, MG = AT4, AT4[:].rearrange("p n c -> p (n c)").rearrange("p (j c) -> p j c", j=8)
            BRr = tA
            otile, ot_ap = VTM, VTM[:].rearrange("p n c -> p (n c)").bitcast(F32)
            P.memset("gpsimd", Yf[:].bitcast(BF16)[:, 0:3072], 0.0, [V2b])
            for hh in range(2):
                pb = bk[hh]
                P.mm(pb.ap[:], sel[:, b, :], garow[:, 512 * hh:512 * hh + 512], True, True, [sel, garow], [pb])
                P.copy("vector", GPb[:, 512 * hh:512 * hh + 512], pb.ap[:], [pb], [GPb])
            for ti in range(4):
                t0 = 512 * ti
                lo = max(t0 - 128, 0)
                hi = min(t0 + 640, S)
                off = lo - (t0 - 128)
                nwin = hi - lo
                for j in range(6):
                    wb = load_w(wsc_in[18 + j])
                    for (c0, cn) in ((0, 384), (384, nwin - 384)):
                        if cn <= 0:
                            continue
                        pb = bk[(j * 2 + (c0 > 0)) % 4]
                        for k in range(8):
                            P.mm(pb.ap[:, 0:cn], wb[:, k, :], hT[:, k, lo + c0:lo + c0 + cn], k == 0, k == 7, [wb, hT], [pb])
                        dstb = QTb if j < 4 else KAb
                        dsta = (QT[:, j, off + c0:off + c0 + cn] if j < 4 else KA[:, j - 4, off + c0:off + c0 + cn])
                        if j % 2:
                            P.act(dsta, pb.ap[:, 0:cn], AF.Copy, [pb], [dstb])
                        else:
                            P.copy("vector", dsta, pb.ap[:, 0:cn], [pb], [dstb])
                for j in range(4):
                    wb = load_w(wsc_in[24 + j])
                    pb = bk[4 + j % 2]
                    for k in range(8):
                        P.mm(pb.ap[:], wb[:, k, :], hT[:, k, t0:t0 + 512], k == 0, k == 7, [wb, hT], [pb])
                    P.act(GA[:, j, :], pb.ap[:], AF.Silu, [pb], [GAb])
                for wbk in range(6):
                    tb0 = t0 - 128 + 128 * wbk
                    if tb0 < 0 or tb0 >= S:
                        continue
                    pb = bk[6 + wbk % 2]
                    for k in range(8):
                        P.mm(pb.ap[:, 0:128], hT[:, k, tb0:tb0 + 128], wv_b[:, k, :], k == 0, k == 7, [hT, wv_b], [pb])
                    for kv in range(2):
                        P.copy("vector", V2[:, wbk, kv, 0, 0:64], pb.ap[:, 64 * kv:64 * kv + 64], [pb], [V2b])
                        P.act(V2[:, wbk, kv, 1, 64:128], pb.ap[:, 64 * kv:64 * kv + 64], AF.Copy, [pb], [V2b])
                for qb in range(4):
                    gq = 4 * ti + qb
                    qc = 128 + 128 * qb
                    slots = [sl_ for sl_ in range(3) if 0 <= gq - 1 + sl_ < NCK]
                    for pr_ in range(4):
                        kv = pr_ // 2
                        for hh in range(2):
                            hr = slice(64 * hh, 64 * hh + 64)
                            ps_ = bk[hh]
                            for sl_ in slots:
                                kc = qc - 128 + 128 * sl_
                                P.mm(ps_.ap[:, 128 * sl_:128 * sl_ + 128], KA[hr, kv, kc:kc + 128], QT[hr, pr_, qc:qc + 128], True, True, [KAb, QTb], [ps_])
                            a_, b_ = 128 * slots[0], 128 * slots[-1] + 128
                            P.act(tE[:, a_:b_], ps_.ap[:, a_:b_], AF.Exp, [ps_], [tE], scale=0.125)
                            P.tt("vector", PT[:, hh, a_:b_], tE[:, a_:b_], ebias[:, 2 * pr_ + hh, :, :].rearrange("p s c -> p (s c)")[:, a_:b_], ALU.mult, [tE, ebias], [PTb])
                        po_ = bk[2 + pr_ % 2]
                        pd_ = bk[4 + pr_ % 2]
                        nmm = 2 * len(slots)
                        idx = 0
                        for hh in range(2):
                            for sl_ in slots:
                                P.mm(po_.ap[:, 0:128], V2[:, qb + sl_, kv, hh, :], PT[:, hh, 128 * sl_:128 * sl_ + 128], idx == 0, idx == nmm - 1, [V2b, PTb], [po_])
                                P.mm(pd_.ap[:, 0:128], onesLR[:, hh, :], PT[:, hh, 128 * sl_:128 * sl_ + 128], idx == 0, idx == nmm - 1, [onesLR, PTb], [pd_])
                                idx += 1
                        P.act(tK[:, 0:128], pd_.ap[:, 0:128], AF.Ln, [pd_, esink], [tK], bias=esink[:, pr_:pr_ + 1])
                        P.act(tK[:, 0:128], tK[:, 0:128], AF.Exp, [tK], [tK], scale=-1.0)
                        P.tt("vector", tK[:, 0:128], tK[:, 0:128], po_.ap[:, 0:128], ALU.mult, [tK, po_], [tK])
                        P.tt("vector", YA[:, pr_, 128 * qb:128 * qb + 128], tK[:, 0:128], GA[:, pr_, 128 * qb:128 * qb + 128], ALU.mult, [tK, GAb], [YAb])
                dump(f"YA{b}_{ti}", YAb, YA, [128, 4, 512], BF16)
                for oc in range(8):
                    wr_ = load_w(wsc_br[oc], 512)
                    pr2 = bk[0 + oc % 2]
                    for k in range(4):
                        P.mm(pr2.ap[:], wr_[:, k, :], YR[:, k, t0:t0 + 512], k == 0, k == 3, [wr_, YR], [pr2])
                    wgr = load_w(wsc_in[28 + oc])
                    pg2 = bk[2 + oc % 2]
                    for k in range(8):
                        P.mm(pg2.ap[:], wgr[:, k, :], hT[:, k, t0:t0 + 512], k == 0, k == 7, [wgr, hT], [pg2])
                    P.act(tE[:], pg2.ap[:], AF.Sigmoid, [pg2], [tE])
                    P.tt("vector", BRr[:], tE[:], pr2.ap[:], ALU.mult, [tE, pr2], [BRr])
                    wa_ = load_w(wsc_br[8 + oc], 512)
                    pa3 = bk[4 + oc % 2]
                    for k in range(4):
                        P.mm(pa3.ap[:], wa_[:, k, :], YA[:, k, :], k == 0, k == 3, [wa_, YAb], [pa3])
                    wga = load_w(wsc_in[36 + oc])
                    pg3 = bk[6 + oc % 2]
                    for k in range(8):
                        P.mm(pg3.ap[:], wga[:, k, :], hT[:, k, t0:t0 + 512], k == 0, k == 7, [wga, hT], [pg3])
                    P.act(tK[:], pg3.ap[:], AF.Sigmoid, [pg3], [tK])
                    P.tt("vector", tK[:], tK[:], pa3.ap[:], ALU.mult, [tK, pa3], [tK])
                    P.tt("gpsimd", MG[:, oc, :], tK[:], BRr[:], ALU.add, [tK, BRr], [MGb])
                for tb in range(4):
                    tok = t0 + 128 * tb
                    xb_ = xt[tb % 2]
                    xa_ = xap[tb % 2]
                    P.dma("sync", lambda e, xa_=xa_, tok=tok, b=b: e.dma_start(out=xa_, in_=x[b, tok:tok + 128, :]), writes=[xb_], key=f"xt{tb % 2}")
                    for hh in range(2):
                        po2 = bk[2 * (tb % 2) + hh]
                        for k in range(8):
                            P.mm(po2.ap[:], MG[:, k, 128 * tb:128 * tb + 128], wout_b[:, k, 512 * hh:512 * hh + 512], k == 0, k == 7, [MGb, wout_b], [po2])
                        P.act(junk[:, 0:512], po2.ap[:], AF.Square, [po2], [junk, st4], accum=st4[:, 4 + hh:5 + hh])
                    P.tt("vector", st4[:, 6:7], st4[:, 4:5], st4[:, 5:6], ALU.add, [st4], [st4])
                    P.act(st4[:, 6:7], st4[:, 6:7], AF.Ln, [st4, eps_t], [st4], scale=1.0 / 1024, bias=eps_t[:, 0:1])
                    P.act(st4[:, 7:8], st4[:, 6:7], AF.Exp, [st4], [st4], scale=-0.5)
                    for hh in range(2):
                        po2 = bk[2 * (tb % 2) + hh]
                        cs2 = slice(512 * hh, 512 * hh + 512)
                        P.stt(ot_ap[:, cs2], po2.ap[:], st4[:, 7:8], GPb[:, cs2], ALU.mult, ALU.mult, [po2, st4, GPb], [otile])
                    P.tt("gpsimd", ot_ap, ot_ap, xa_, ALU.add, [otile, xb_], [otile])
                    P.dma("gpsimd", lambda e, tok=tok, b=b: e.dma_start(out=out[b, tok:tok + 128, :], in_=ot_ap), reads=[otile], key="ost")
        P.final_wait("gpsimd", [VTM] if stop_after > 5 else [hT])
        P.emit()
    return nc, dbg_out


def prep_inputs(inputs, nseq=4, ncores=8):
    f = lambda a: np.ascontiguousarray(np.asarray(a, dtype=np.float32))
    w_in = f(inputs["w_in"])[0]
    W = 512
    cols = []
    cols.append(np.arange(4 * W, 4 * W + 128))
    cols.append(np.arange(4 * W + 128, 4 * W + 256))
    for hp in range(4):
        for j in range(4):
            cols.append(np.arange(j * W + 128 * hp, j * W + 128 * hp + 128))
    A0 = 2304
    for j in range(4):
        cols.append(np.arange(A0 + 128 * j, A0 + 128 * j + 128))
    cols.append(np.concatenate([np.arange(A0 + 512, A0 + 576)] * 2))
    cols.append(np.concatenate([np.arange(A0 + 576, A0 + 640)] * 2))
    for j in range(4):
        cols.append(np.arange(A0 + 768 + 128 * j, A0 + 768 + 128 * j + 128))
    G0 = A0 + 1280
    for j in range(16):
        cols.append(np.arange(G0 + 128 * j, G0 + 128 * j + 128))
    assert len(cols) == NCH
    w_inp = np.stack([w_in[:, c].reshape(8, 128, 128).transpose(1, 0, 2).reshape(128, 1024) for c in cols])
    w_vatt = w_in[:, A0 + 640:A0 + 768].reshape(8, 128, 128).transpose(1, 0, 2).reshape(128, 1024)
    mu = f(inputs["mu_shift"])[0]
    rw_cols = np.concatenate(cols[:18])
    mu_l = mu[:, rw_cols].reshape(2, 18, 128).transpose(2, 0, 1)
    pl = lambda v: f(v).reshape(4, 128).T
    w0 = np.stack([pl(f(inputs["w0"])[0, d]) for d in range(2)], axis=1)
    a0 = np.stack([pl(f(inputs["a0"])[0, d]) for d in range(2)], axis=1)
    sink = f(inputs["sink"])[0]
    sink_l = np.repeat(sink.reshape(4, 2, 1), 64, axis=2).reshape(4, 128).T
    pvec = np.stack([pl(inputs["k_k"][0]), pl(inputs["k_a"][0]), pl(np.asarray(inputs["r_k"])[0].reshape(512)),
                     pl(inputs["gn_w"][0]), pl(inputs["gn_b"][0]), sink_l], axis=1)
    b_ada = f(inputs["b_ada"])[0]
    common = {
        "w_ada": f(np.stack([f(inputs["w_ada"])[0][:, 128 * j:128 * j + 128].reshape(8, 128, 128).transpose(1, 0, 2).reshape(128, 1024) for j in range(24)])),
        "b_ada": f(b_ada.reshape(24, 128).T),
        "b_gate": f(np.tile(b_ada[2048:3072][None, :], (nseq, 1))),
        "gpost_r": f(np.tile(f(inputs["g_post"])[0][None, :], (nseq, 1))),
        "gpre": f(f(inputs["g_pre"])[0].reshape(8, 128).T),
        "w_inp": f(w_inp), "w_vatt": f(w_vatt), "mu": f(mu_l), "w0": f(w0), "a0": f(a0),
        "w_up": f(f(inputs["w_up"])[0].reshape(128, 512)), "a_up": f(f(inputs["a_up"])[0].reshape(128, 512)),
        "pvec": f(pvec),
        "w_brr": f(f(inputs["w_br_rwkv"])[0].reshape(4, 128, 8, 128).transpose(2, 1, 0, 3).reshape(8, 128, 512)),
        "w_bra": f(f(inputs["w_br_att"])[0].reshape(4, 128, 8, 128).transpose(2, 1, 0, 3).reshape(8, 128, 512)),
        "w_outp": f(f(inputs["w_out"])[0].reshape(8, 128, 1024).transpose(1, 0, 2).reshape(128, 8192)),
    }
    xs = f(inputs["x"])
    c = f(inputs["c"])
    maps = []
    for i in range(ncores):
        m = dict(common)
        m["x"] = xs[nseq * i:nseq * i + nseq]
        m["cT"] = f(c[nseq * i:nseq * i + nseq].T.reshape(8, 128, nseq).transpose(1, 0, 2))
        maps.append(m)
    return maps


def kernel(**inputs):
    nc, _ = build(4)
    maps = prep_inputs(inputs, 4, 8)
    res = run_bass_kernel_spmd(nc, maps, core_ids=list(range(8)))
    return np.concatenate([np.asarray(r["out"]) for r in res.results], axis=0).astype(np.float32)
```

```python
import math
import numpy as np
from contextlib import ExitStack
import concourse.bass as bass
import concourse.mybir as mybir
from concourse.bass_utils import run_bass_kernel_spmd

F32 = mybir.dt.float32
BF16 = mybir.dt.bfloat16
I32 = mybir.dt.int32
AF = mybir.ActivationFunctionType
ALU = mybir.AluOpType

ENGS = ["sync", "scalar", "vector", "gpsimd", "tensor"]
S = 2048
NCK = 16
NCH = 44
EXPM05 = math.exp(-0.5)


class Buf:
    def __init__(self, P, t, name, atoms):
        self.P = P
        self.t = t
        self.name = name
        self.own = atoms
        self.kids = {}

    def __getitem__(self, idx):
        return self.t[idx]

    def atoms(self):
        a = list(self.own)
        for k in self.kids.values():
            a += k.own
        return a

    def sub(self, key):
        if key not in self.kids:
            aid = self.P.new_atom()
            st = self.P.state
            st[aid] = {"w": st[self.own[0]]["w"], "r": dict(st[self.own[0]]["r"])}
            self.kids[key] = Buf(self.P, self.t, f"{self.name}.{key}", [aid])
        return self.kids[key]


class Prog:
    def __init__(self, nc, es):
        self.nc = nc
        self.es = es
        self.lists = {e: [] for e in ENGS}
        self.cnt = {e: 0 for e in ENGS}
        self.seen = {e: {} for e in ENGS}
        self.sems = {}
        for e in ENGS:
            self.sems[e] = es.enter_context(nc.semaphore("s_" + e))
        self.dma_cnt = {}
        self.nbuf = 0
        self.state = {}
        self.natom = 0
        self.dbg = {}

    def new_atom(self):
        self.natom += 1
        self.state[self.natom] = {"w": None, "r": {}}
        return self.natom

    def sb(self, shape, dt, name=None):
        self.nbuf += 1
        name = (name or "b") + f"_{self.nbuf}"
        t = self.es.enter_context(self.nc.sbuf_tensor(name, list(shape), dt))
        return Buf(self, t, name, [self.new_atom()])

    def _deps(self, eng, reads, writes):
        waits = {}

        def need(ev):
            if ev is None:
                return
            k, v = ev
            if eng == "tensor" and k == "tensor":
                return
            if self.seen[eng].get(k, 0) >= v:
                return
            waits[k] = max(waits.get(k, 0), v)

        for b in reads:
            for a in b.atoms():
                need(self.state[a]["w"])
        for b in writes:
            for a in b.atoms():
                st = self.state[a]
                need(st["w"])
                for k, v in st["r"].items():
                    need((k, v))
        for k, v in waits.items():
            self.seen[eng][k] = v
        return waits

    def _mark(self, key, c, reads, writes):
        for b in reads:
            for a in b.atoms():
                self.state[a]["r"][key] = c
        for b in writes:
            for a in b.atoms():
                self.state[a]["w"] = (key, c)
                self.state[a]["r"] = {}

    def op(self, eng, fn, reads=(), writes=()):
        waits = self._deps(eng, reads, writes)
        self.cnt[eng] += 1
        self.lists[eng].append((fn, waits, (eng, 1)))
        self._mark(eng, self.cnt[eng], reads, writes)

    def dma(self, eng, fn, reads=(), writes=(), key=None):
        if key not in self.sems:
            self.sems[key] = self.es.enter_context(self.nc.semaphore("s_" + key))
            self.dma_cnt[key] = 0
        waits = self._deps(eng, reads, writes)
        self.dma_cnt[key] += 16
        self.lists[eng].append((fn, waits, (key, 16)))
        self._mark(key, self.dma_cnt[key], reads, writes)

    def final_wait(self, eng, bufs):
        waits = self._deps(eng, bufs, bufs)
        self.lists[eng].append((None, waits, None))

    def emit(self):
        with self.nc.Block() as block:
            def run(engname):
                def f(e):
                    for fn, waits, inc in self.lists[engname]:
                        for k, v in waits.items():
                            e.wait_ge(self.sems[k], v)
                        if fn is None:
                            continue
                        fn(e).then_inc(self.sems[inc[0]], inc[1])
                return f
            block.sync(run("sync"))
            block.scalar(run("scalar"))
            block.vector(run("vector"))
            block.gpsimd(run("gpsimd"))
            block.tensor(run("tensor"))

    def act(self, out, in_, func, r, w, scale=1.0, bias=None, accum=None):
        kw = dict(out=out, in_=in_, func=func, scale=scale)
        if bias is not None:
            kw["bias"] = bias
        if accum is not None:
            kw["accum_out"] = accum
        self.op("scalar", lambda e: e.activation(**kw), r, w)

    def copy(self, eng, out, in_, r, w):
        self.op(eng, lambda e: e.tensor_copy(out=out, in_=in_), r, w)

    def tt(self, eng, out, in0, in1, op, r, w):
        self.op(eng, lambda e: e.tensor_tensor(out=out, in0=in0, in1=in1, op=op), r, w)

    def ts(self, eng, out, in0, s1, op0, r, w, s2=None, op1=None):
        if op1 is None:
            self.op(eng, lambda e: e.tensor_scalar(out=out, in0=in0, scalar1=s1, scalar2=None, op0=op0), r, w)
        else:
            self.op(eng, lambda e: e.tensor_scalar(out=out, in0=in0, scalar1=s1, scalar2=s2, op0=op0, op1=op1), r, w)

    def stt(self, out, in0, scalar, in1, op0, op1, r, w):
        self.op("vector", lambda e: e.scalar_tensor_tensor(out=out, in0=in0, scalar=scalar, in1=in1, op0=op0, op1=op1), r, w)

    def mm(self, out, lhsT, rhs, start, stop, r, w):
        self.op("tensor", lambda e: e.matmul(out, lhsT=lhsT, rhs=rhs, start=start, stop=stop), r, w)

    def tr(self, out, in_, ident, r, w):
        self.op("tensor", lambda e: e.transpose(out=out, in_=in_, identity=ident), r, w)

    def memset(self, eng, ap, val, w):
        self.op(eng, lambda e: e.memset(ap, val), (), w)


def build(nseq=4, dbg=(), stop_after=99):
    nc = bass.Bass("TRN2", target_bir_lowering=False)

    def din(name, shape, dt=F32):
        return nc.dram_tensor(name, list(shape), dt, kind="ExternalInput").ap()

    x = din("x", [nseq, S, 1024])
    cT = din("cT", [128, 8, nseq])
    w_ada = din("w_ada", [24, 128, 1024])
    b_ada = din("b_ada", [128, 24])
    b_gate = din("b_gate", [nseq, 1024])
    gpost_r = din("gpost_r", [nseq, 1024])
    gpre = din("gpre", [128, 8])
    w_inp = din("w_inp", [NCH, 128, 1024])
    w_vatt = din("w_vatt", [128, 1024])
    mu = din("mu", [128, 2, 18])
    w0 = din("w0", [128, 2, 4])
    a0 = din("a0", [128, 2, 4])
    w_up = din("w_up", [128, 512])
    a_up = din("a_up", [128, 512])
    pvec = din("pvec", [128, 6, 4])
    w_brr = din("w_brr", [8, 128, 512])
    w_bra = din("w_bra", [8, 128, 512])
    w_outp = din("w_outp", [128, 8192])
    out = nc.dram_tensor("out", [nseq, S, 1024], F32, kind="ExternalOutput").ap()
    dbg_out = {}

    with ExitStack() as es:
        P = Prog(nc, es)
        psA_t = es.enter_context(nc.psum_tensor("psA", [128, 2048], F32))
        psB_t = es.enter_context(nc.psum_tensor("psB", [128, 2048], F32))
        bank_atoms = [P.new_atom() for _ in range(8)]

        class PsV:
            def __init__(self, b0, n=1):
                self.b0, self.n = b0, n
                self.own = bank_atoms[b0:b0 + n]
                t = psA_t if b0 < 4 else psB_t
                c0 = (b0 % 4) * 512
                self.ap = t[:, c0:c0 + 512 * n]
                self.bf = t[:, c0:c0 + 512 * n].bitcast(BF16)

            def atoms(self):
                return self.own

        bk = [PsV(i) for i in range(8)]
        big = [PsV(0, 4), PsV(4, 4)]

        def dump(name, buf, ap, shape, dt=F32):
            if name not in dbg:
                return
            d = nc.dram_tensor("dbg_" + name, list(shape), dt, kind="ExternalOutput").ap()
            dbg_out[name] = d
            P.dma("gpsimd", lambda e: e.dma_start(out=d, in_=ap), reads=[buf], key="dbg_" + name)
            P.final_wait("gpsimd", [buf])

        TQ = 512
        tA = P.sb([128, TQ], F32, "tA")
        tLW = P.sb([128, TQ], F32, "tLW")
        tP = P.sb([128, 4, 128], F32, "tP")
        tDL = P.sb([128, TQ], F32, "tDL")
        tE = P.sb([128, TQ], F32, "tE")
        tK = P.sb([128, TQ], F32, "tK")
        ones_f = P.sb([128, 128], F32, "ones_f")
        P.memset("gpsimd", ones_f[:], 1.0, [ones_f])
        tmpm = P.sb([128, 128], F32, "tmpm")

        def aff(dst_buf, dst_ap, pattern_step, cm, cmp):
            P.op("gpsimd", lambda e: e.affine_select(out=tmpm[:], in_=ones_f[:], pattern=[[pattern_step, 128]],
                                                     compare_op=cmp, fill=0.0, base=0, channel_multiplier=cm),
                 [ones_f], [tmpm])
            P.copy("gpsimd", dst_ap, tmpm[:], [tmpm], [dst_buf])

        ident = P.sb([128, 128], BF16, "ident")
        aff(ident, ident[:], -1, 1, ALU.is_equal)
        maskT = [P.sb([128, 512], BF16, "maskTf"), P.sb([128, 512], BF16, "maskTb")]
        maskA = [P.sb([128, 128], BF16, "maskAf"), P.sb([128, 128], BF16, "maskAb")]
        SU = (1, -1, ALU.is_gt)
        IU = (1, -1, ALU.is_ge)
        SL = (-1, 1, ALU.is_gt)
        IL = (-1, 1, ALU.is_ge)
        for q, m in enumerate([SU, IU, SU, IU]):
            aff(maskT[0], maskT[0][:, 128 * q:128 * q + 128], *m)
        for q, m in enumerate([SL, IL, SL, IL]):
            aff(maskT[1], maskT[1][:, 128 * q:128 * q + 128], *m)
        aff(maskA[0], maskA[0][:], *SL)
        aff(maskA[1], maskA[1][:], *SU)
        blk32 = P.sb([128, 128], BF16, "blk32")
        off64 = P.sb([128, 256], BF16, "off64")
        off128 = P.sb([128, 128], BF16, "off128")
        for t_ in (blk32, off64, off128):
            P.memset("gpsimd", t_[:], 0.0, [t_])
        for q in range(4):
            P.memset("gpsimd", blk32[32 * q:32 * q + 32, 32 * q:32 * q + 32], 1.0, [blk32])
            q2 = q ^ 1
            for rpt in range(2):
                P.memset("gpsimd", off64[32 * q:32 * q + 32, 128 * rpt + 32 * q2:128 * rpt + 32 * q2 + 32], 1.0, [off64])
        P.memset("gpsimd", off128[0:64, 64:128], 1.0, [off128])
        P.memset("gpsimd", off128[64:128, 0:64], 1.0, [off128])
        mM = [P.sb([128, 128], BF16, "mMf"), P.sb([128, 128], BF16, "mMb")]
        for d_ in range(2):
            P.tt("gpsimd", mM[d_][:], maskA[d_][:], blk32[:], ALU.mult, [maskA[d_], blk32], [mM[d_]])
        ones_blk = P.sb([128, 128], F32, "ones_blk")
        P.memset("gpsimd", ones_blk[:], 1.0, [ones_blk])
        P.memset("gpsimd", ones_blk[0:64, 64:128], 0.0, [ones_blk])
        P.memset("gpsimd", ones_blk[64:128, 0:64], 0.0, [ones_blk])
        onesLR = P.sb([128, 2, 128], BF16, "onesLR")
        P.memset("gpsimd", onesLR[:], 0.0, [onesLR])
        P.memset("gpsimd", onesLR[:, 0, 0:64], 1.0, [onesLR])
        P.memset("gpsimd", onesLR[:, 1, 64:128], 1.0, [onesLR])
        cmask = P.sb([128, 4, 128], BF16, "cmask")
        P.memset("gpsimd", cmask[:], 1.0, [cmask])
        P.memset("gpsimd", cmask[:, :, 0:1], 0.0, [cmask])
        sel = P.sb([nseq, nseq, 128], F32, "sel")
        P.memset("gpsimd", sel[:], 1.0, [sel])
        P.op("gpsimd", lambda e: e.affine_select(out=sel[:], in_=sel[:], pattern=[[-1, nseq], [0, 128]],
                                                 compare_op=ALU.is_equal, fill=0.0, base=0, channel_multiplier=1),
             [sel], [sel])
        ebias = P.sb([128, 8, 3, 128], BF16, "ebias")
        di_ap = tLW[:, 0:384].bitcast(I32).rearrange("p (s c) -> p s c", s=3)
        df_ap = tE[:, 0:384].rearrange("p (s c) -> p s c", s=3)
        et_ap = tK[:, 0:384].rearrange("p (s c) -> p s c", s=3)
        vm_ap = tA[:, 0:256].rearrange("p (s c) -> p s c", s=2)
        P.op("gpsimd", lambda e: e.iota(di_ap[:, 0, :], pattern=[[1, 128]], base=128, channel_multiplier=-1), (), [tLW])
        P.op("gpsimd", lambda e: e.iota(di_ap[:, 1, :], pattern=[[1, 128]], base=0, channel_multiplier=-1), (), [tLW])
        P.op("gpsimd", lambda e: e.iota(di_ap[:, 2, :], pattern=[[-1, 128]], base=128, channel_multiplier=1), (), [tLW])
        P.copy("vector", df_ap, di_ap, [tLW], [tE])
        P.act(df_ap[:, 1, :], df_ap[:, 1, :], AF.Abs, [tE], [tE])
        aff(tA, vm_ap[:, 0, :], *IL)
        aff(tA, vm_ap[:, 1, :], *IU)
        for h in range(8):
            slope = 2.0 ** (-(h + 1))
            P.act(et_ap, df_ap, AF.Exp, [tE], [tK], scale=-slope)
            P.tt("vector", ebias[:, h, 0, :], et_ap[:, 0, :], vm_ap[:, 0, :], ALU.mult, [tK, tA], [ebias])
            P.copy("vector", ebias[:, h, 1, :], et_ap[:, 1, :], [tK], [ebias])
            P.tt("vector", ebias[:, h, 2, :], et_ap[:, 2, :], vm_ap[:, 1, :], ALU.mult, [tK, tA], [ebias])

        def load_small(src, shape, name):
            b = P.sb(shape, F32, name)
            P.dma("sync", lambda e: e.dma_start(out=b[:], in_=src), writes=[b], key="ld_" + name)
            return b

        mu_t = load_small(mu, [128, 2, 18], "mu")
        w0_t = load_small(w0, [128, 2, 4], "w0")
        a0_t = load_small(a0, [128, 2, 4], "a0")
        pv = load_small(pvec, [128, 6, 4], "pv")
        gpre_t = load_small(gpre, [128, 8], "gpre")
        bada_t = load_small(b_ada, [128, 24], "bada")
        cT_t = load_small(cT, [128, 8, nseq], "cT")
        c0_t = P.sb([128, 18], F32, "c0")
        P.tt("vector", c0_t[:], mu_t[:, 0, :], mu_t[:, 1, :], ALU.add, [mu_t], [c0_t])
        P.ts("vector", c0_t[:], c0_t[:], -1.0, ALU.mult, [c0_t], [c0_t], s2=1.0, op1=ALU.add)
        omka = P.sb([128, 4], F32, "omka")
        P.ts("vector", omka[:], pv[:, 1, :], -1.0, ALU.mult, [pv], [omka], s2=1.0, op1=ALU.add)
        a0h = P.sb([128, 2, 4], F32, "a0h")
        w0h = P.sb([128, 2, 4], F32, "w0h")
        P.ts("vector", a0h[:], a0_t[:], 0.5, ALU.mult, [a0_t], [a0h])
        P.ts("vector", w0h[:], w0_t[:], 0.5, ALU.mult, [w0_t], [w0h])
        kah = P.sb([128, 4], F32, "kah")
        omkah = P.sb([128, 4], F32, "omkah")
        P.ts("vector", kah[:], pv[:, 1, :], 0.5, ALU.mult, [pv], [kah])
        P.ts("vector", omkah[:], pv[:, 1, :], -0.5, ALU.mult, [pv], [omkah], s2=1.0, op1=ALU.add)
        esink = P.sb([128, 4], F32, "esink")
        P.act(esink[:], pv[:, 5, :], AF.Exp, [pv], [esink])
        eps_t = P.sb([128, 1], F32, "eps")
        P.memset("vector", eps_t[:], 1e-6, [eps_t])
        gneps_t = P.sb([128, 1], F32, "gneps")
        P.memset("vector", gneps_t[:], 64e-5, [gneps_t])

        hT = P.sb([128, 8, S], BF16, "hT")
        hTf = hT[:].rearrange("p k s -> p (k s)").bitcast(F32)
        NSS = 8
        NW = 3
        wbf = [P.sb([128, 8, 128], BF16, f"wbf{i}") for i in range(NW)]
        wctr = [0]

        def load_cast(dst_buf, dst_ap, src_ap, ncols):
            c = wctr[0]
            wctr[0] += 1
            sb_ = hT.sub(("st", c % NSS))
            sap = hTf[:, 1024 * (c % NSS):1024 * (c % NSS) + ncols]
            P.dma("sync", lambda e: e.dma_start(out=sap, in_=src_ap), writes=[sb_], key=f"wst{c % NSS}")
            if c % 2:
                P.act(dst_ap, sap, AF.Copy, [sb_], [dst_buf])
            else:
                P.copy("vector", dst_ap, sap, [sb_], [dst_buf])

        wsc_in = nc.dram_tensor("wsc_in", [NCH, 128, 1024], BF16).ap()
        wsc_br = nc.dram_tensor("wsc_br", [16, 128, 512], BF16).ap()
        WSC = Buf(P, None, "wsc", [P.new_atom()])
        wbc = [0]
        for ci in range(NCH + 16):
            wb = wbf[wbc[0] % NW]
            wbc[0] += 1
            if ci < NCH:
                srcw, dstw, ncw = w_inp[ci], wsc_in[ci], 1024
            elif ci < NCH + 8:
                srcw, dstw, ncw = w_brr[ci - NCH], wsc_br[ci - NCH], 512
            else:
                srcw, dstw, ncw = w_bra[ci - NCH - 8], wsc_br[ci - NCH], 512
            wflat = wb[:].rearrange("p k c -> p (k c)")[:, 0:ncw]
            load_cast(wb, wflat, srcw, ncw)
            P.dma("sync", lambda e, dstw=dstw, wflat=wflat: e.dma_start(out=dstw, in_=wflat), reads=[wb], writes=[WSC], key="wsc_w")

        def load_w(src_ap, ncols=1024):
            i = wbc[0] % NW
            wbc[0] += 1
            wb = wbf[i]
            P.dma("sync", lambda e: e.dma_start(out=wb[:].rearrange("p k c -> p (k c)")[:, 0:ncols], in_=src_ap), reads=[WSC], writes=[wb], key=f"wbf{i}")
            return wb

        wout_b = P.sb([128, 8, 1024], BF16, "wout_b")
        for k in range(8):
            load_cast(wout_b, wout_b[:, k, :], w_outp[:, 1024 * k:1024 * k + 1024], 1024)
        wv_b = P.sb([128, 8, 128], BF16, "wv_b")
        load_cast(wv_b, wv_b[:].rearrange("p k c -> p (k c)"), w_vatt, 1024)
        wup_b = P.sb([128, 512], BF16, "wupb")
        aup_b = P.sb([128, 512], BF16, "aupb")
        load_cast(wup_b, wup_b[:], w_up, 512)
        load_cast(aup_b, aup_b[:], a_up, 512)

        Kf = P.sb([128, S], F32, "Kf")
        Yf = P.sb([128, S], F32, "Yf")
        bgate_t = Kf
        gpostr_t = Kf
        P.dma("sync", lambda e: e.dma_start(out=Kf[0:nseq, 0:1024], in_=b_gate), writes=[Kf], key="ld_bgate")
        P.dma("sync", lambda e: e.dma_start(out=Kf[0:nseq, 1024:2048], in_=gpost_r), writes=[Kf], key="ld_bgate")
        cond = P.sb([128, 8, nseq], F32, "cond")
        P.act(cond[:], cT_t[:], AF.Silu, [cT_t], [cond])
        adaT = P.sb([128, 24, nseq], F32, "adaT")
        garow = P.sb([nseq, 1024], F32, "garow")
        for jj in range(24):
            i_ = jj % 2
            wa = Yf.sub(i_)
            wa_ap = Yf[:, 1024 * i_:1024 * i_ + 1024]
            P.dma("sync", lambda e, wa_ap=wa_ap, jj=jj: e.dma_start(out=wa_ap, in_=w_ada[jj]), writes=[wa], key=f"wada{i_}")
            pb = bk[jj % 2]
            for k in range(8):
                P.mm(pb.ap[:, 0:nseq], wa_ap[:, 128 * k:128 * k + 128], cond[:, k, :], k == 0, k == 7, [wa, cond], [pb])
            P.act(adaT[:, jj, :], pb.ap[:, 0:nseq], AF.Identity, [pb, bada_t], [adaT], bias=bada_t[:, jj:jj + 1])
            if jj >= 16:
                pr = bk[2 + jj % 2]
                for k in range(8):
                    P.mm(pr.ap[0:nseq, 0:128], cond[:, k, :], wa_ap[:, 128 * k:128 * k + 128], k == 0, k == 7, [wa, cond], [pr])
                c0 = 128 * (jj - 16)
                P.tt("vector", garow[:, c0:c0 + 128], pr.ap[0:nseq, 0:128], bgate_t[0:nseq, c0:c0 + 128], ALU.add, [pr, bgate_t], [garow])
        P.tt("vector", garow[:], garow[:], gpostr_t[0:nseq, 1024:2048], ALU.mult, [garow, gpostr_t], [garow])
        preA = P.sb([128, 8, nseq], F32, "preA")
        P.ts("vector", preA[:], adaT[:, 8:16, :], 1.0, ALU.add, [adaT], [preA])
        for b in range(nseq):
            P.tt("vector", preA[:, :, b], preA[:, :, b], gpre_t[:], ALU.mult, [preA, gpre_t], [preA])
        dump("adaT", adaT, adaT[:], [128, 24, nseq])
        dump("garow", garow, garow[:], [nseq, 1024])

        TW = P.sb([128, S], BF16, "TW")
        AL = P.sb([128, S], BF16, "AL")
        YR = P.sb([128, 4, S], BF16, "YR")
        xn = P.sb([128, 1024], BF16, "xn")
        st4 = P.sb([128, 8], F32, "st4")
        GPb = P.sb([128, 1024], BF16, "GPb")
        Rb = P.sb([128, S], BF16, "Rb")
        Vb = P.sb([128, S], BF16, "Vb")
        KKf = P.sb([128, S], BF16, "KKb")
        SG = P.sb([128, S], BF16, "SG")
        xt = [Rb, Vb]
        xap = [Rb[:].bitcast(F32), Vb[:].bitcast(F32)]
        junk = xn
        BON = P.sb([128, S], BF16, "BON")
        VTM = P.sb([128, NCK, 128], BF16, "VTM")
        AR = P.sb([128, NCK, 256], BF16, "AR")
        KT = P.sb([128, S], BF16, "KT")
        BT = P.sb([128, S], BF16, "BT")
        KTM = P.sb([128, 4, 128], BF16, "KTM")
        BTM = P.sb([128, 4, 128], BF16, "BTM")
        Ffac = P.sb([128, NCK], F32, "Ffac")
        P63 = P.sb([128, NCK], F32, "P63")
        Qd = P.sb([128, NCK], F32, "Qd")
        HC = 4
        AT4 = P.sb([128, 2 * HC, 512], BF16, "AT4")
        TT = P.sb([128, 2 * HC, 128], BF16, "TT")
        stgs = [[P.sb([128, 384], BF16, f"stg{i}_{j}") for j in range(2)] for i in range(8)]
        AFt = [P.sb([128, 128], BF16, f"AF{i}") for i in range(8)]
        T64 = [P.sb([128, 256], BF16, f"T64{i}") for i in range(8)]
        HS = [P.sb([128, 64], BF16, f"HS{i}") for i in range(2)]
        Wsb = P.sb([128, 128], BF16, "Wsb")
        Usb = P.sb([128, 128], BF16, "Usb")

        def proj_full(wb, pbig):
            for i in range(4):
                for k in range(8):
                    P.mm(pbig.ap[:, 512 * i:512 * i + 512], wb[:, k, :], hT[:, k, 512 * i:512 * i + 512], k == 0, k == 7, [wb, hT], [pbig])

        def shift_evac(pbig, ci, dst, dst_ap_fn):
            P.act(dst_ap_fn(0, S), pbig.ap[:, 0:S], AF.Copy, [pbig, c0_t], [dst], scale=c0_t[:, ci:ci + 1])
            P.stt(dst_ap_fn(1, S), pbig.ap[:, 0:S - 1], mu_t[:, 0, ci:ci + 1], dst_ap_fn(1, S), ALU.mult, ALU.add, [pbig, mu_t, dst], [dst])
            P.stt(dst_ap_fn(0, S - 1), pbig.ap[:, 1:S], mu_t[:, 1, ci:ci + 1], dst_ap_fn(0, S - 1), ALU.mult, ALU.add, [pbig, mu_t, dst], [dst])

        pcnt = [0]

        def next_big():
            pcnt[0] += 1
            return big[pcnt[0] % 2]

        for b in range(nseq):
            for tb in range(16):
                xb_ = xt[tb % 2]
                xa_ = xap[tb % 2]
                P.dma("sync", lambda e, xa_=xa_, tb=tb, b=b: e.dma_start(out=xa_, in_=x[b, 128 * tb:128 * tb + 128, :]), writes=[xb_], key=f"xt{tb % 2}")
                P.act(junk[:], xa_, AF.Square, [xb_], [junk, st4], accum=st4[:, 0:1])
                P.act(st4[:, 1:2], st4[:, 0:1], AF.Ln, [st4, eps_t], [st4], scale=1.0 / 1024, bias=eps_t[:, 0:1])
                P.act(st4[:, 2:3], st4[:, 1:2], AF.Exp, [st4], [st4], scale=-0.5)
                P.ts("vector", xn[:], xa_, st4[:, 2:3], ALU.mult, [xb_, st4], [xn])
                pb = bk[tb % 2]
                for k in range(8):
                    P.tr(pb.bf[:, 128 * k:128 * k + 128], xn[:, 128 * k:128 * k + 128], ident[:], [xn, ident], [pb])
                for k in range(8):
                    P.act(hT[:, k, 128 * tb:128 * tb + 128], pb.bf[:, 128 * k:128 * k + 128], AF.Identity, [pb, preA, adaT], [hT],
                          scale=preA[:, k, b:b + 1], bias=adaT[:, k, b:b + 1])
            dump(f"hT{b}", hT, hT[:], [128, 8, S], BF16)
            if stop_after <= 1:
                continue
            for ci, (dst, fn) in enumerate([(TW, AF.Tanh), (AL, AF.Copy)]):
                wb = load_w(wsc_in[ci])
                pbig = next_big()
                proj_full(wb, pbig)
                tS = Kf
                shift_evac(pbig, ci, tS, lambda a, c: tS[:, a:c])
                P.act(dst[:], tS[:], fn, [tS], [dst])
            dump(f"TW{b}", TW, TW[:], [128, S], BF16)
            dump(f"AL{b}", AL, AL[:], [128, S], BF16)
            if stop_after <= 2:
                continue
            for hp in range(4):
                for j, name in enumerate("rkvg"):
                    ci = 2 + 4 * hp + j
                    wb = load_w(wsc_in[ci])
                    pbig = next_big()
                    proj_full(wb, pbig)
                    if name == "k":
                        shift_evac(pbig, ci, Kf, lambda a, c: Kf[:, a:c])
                    else:
                        tS = Yf
                        shift_evac(pbig, ci, tS, lambda a, c: tS[:, a:c])
                        if name == "r":
                            P.act(Rb[:], tS[:], AF.Copy, [tS], [Rb])
                        elif name == "v":
                            P.act(Vb[:], tS[:], AF.Copy, [tS], [Vb])
                        else:
                            P.act(SG[:], tS[:], AF.Silu, [tS], [SG])
                P.ts("vector", Yf[:], Kf[:], pv[:, 0, hp:hp + 1], ALU.mult, [Kf, pv], [Yf])
                for i in range(4):
                    sl = slice(512 * i, 512 * i + 512)
                    P.act(tE[:], Yf[:, sl], AF.Square, [Yf], [tE])
                    pb = bk[i % 2]
                    P.mm(pb.ap[:], ones_blk[:], tE[:], True, True, [ones_blk, tE], [pb])
                    P.ts("vector", tK[:], pb.ap[:], 1e-24, ALU.max, [pb], [tK])
                    P.act(tK[:], tK[:], AF.Ln, [tK], [tK])
                    P.act(tK[:], tK[:], AF.Exp, [tK], [tK], scale=-0.5)
                    P.tt("vector", KKf[:, sl], Yf[:, sl], tK[:], ALU.mult, [Yf, tK], [KKf])
                dump(f"KK{b}_{hp}", KKf, KKf[:], [128, S], BF16)
                dump(f"R{b}_{hp}", Rb, Rb[:], [128, S], BF16)
                for g4 in range(4):
                    pb = bk[2 + g4 % 2]
                    for q in range(4):
                        n = 4 * g4 + q
                        P.tr(pb.bf[:, 128 * q:128 * q + 128], Vb[:, 128 * n:128 * n + 128], ident[:], [Vb, ident], [pb])
                    P.copy("vector", VTM[:, 4 * g4:4 * g4 + 4, :], pb.bf[:, 0:512].rearrange("p (q c) -> p q c", q=4), [pb], [VTM])
                for d in range(2):
                    rows = slice(64 * d, 64 * d + 64)
                    for i in range(4):
                        sl = slice(512 * i, 512 * i + 512)
                        pa = bk[0 + i % 2]
                        pw = bk[2 + i % 2]
                        P.mm(pa.ap[:], aup_b[rows, 128 * hp:128 * hp + 128], AL[rows, sl], True, True, [aup_b, AL], [pa])
                        P.mm(pw.ap[:], wup_b[rows, 128 * hp:128 * hp + 128], TW[rows, sl], True, True, [wup_b, TW], [pw])
                        P.act(tA[:], pa.ap[:], AF.Tanh, [pa, a0h], [tA], scale=0.5, bias=a0h[:, d, hp:hp + 1])
                        P.act(tLW[:], pw.ap[:], AF.Tanh, [pw, w0h], [tLW], scale=0.5, bias=w0h[:, d, hp:hp + 1])
                        P.ts("vector", tLW[:], tLW[:], -0.5 * EXPM05, ALU.mult, [tLW], [tLW], s2=-0.5 * EXPM05, op1=ALU.add)
                        P.op("vector", lambda e, i=i: e.tensor_tensor_scan(out=tP[:].rearrange("p n c -> p (n c)"),
                                                                          data0=cmask[:].rearrange("p n c -> p (n c)"),
                                                                          data1=tLW[:], initial=0.0, op0=ALU.mult, op1=ALU.add),
                             [cmask, tLW], [tP])
                        P.copy("vector", P63[:, 4 * i:4 * i + 4], tP[:, :, 63], [tP], [P63])
                        P.tt("vector", Qd[:, 4 * i:4 * i + 4], tP[:, :, 127], tP[:, :, 63], ALU.subtract, [tP], [Qd])
                        P.tt("vector", tP[:], tP[:], P63[:, 4 * i:4 * i + 4].unsqueeze(2).to_broadcast([128, 4, 128]), ALU.subtract, [tP, P63], [tP])
                        tPf = tP[:].rearrange("p n c -> p (n c)")
                        P.tt("vector", tDL[:], tPf, tLW[:], ALU.subtract, [tP, tLW], [tDL])
                        ARv = AR[:, 4 * i:4 * i + 4, :]
                        P.ts("vector", tK[:], tA[:], kah[:, hp:hp + 1], ALU.mult, [tA, kah, omkah], [tK], s2=omkah[:, hp:hp + 1], op1=ALU.add)
                        P.tt("vector", tK[:], tK[:], Kf[:, sl], ALU.mult, [tK, Kf], [tK])
                        P.stt(tE[:], tK[:], pv[:, 2, hp:hp + 1], Rb[:, sl], ALU.mult, ALU.mult, [tK, pv, Rb], [tE])
                        pbn = bk[4 + i % 2]
                        P.mm(pbn.ap[:], ones_blk[:], tE[:], True, True, [ones_blk, tE], [pbn])
                        if d == 0:
                            P.copy("vector", BON[:, sl], pbn.ap[:], [pbn], [BON])
                        else:
                            P.tt("vector", BON[:, sl], BON[:, sl], pbn.ap[:], ALU.add, [pbn, BON], [BON])
                        P.ts("gpsimd", tA[:], tA[:], 0.5, ALU.mult, [tA], [tA], s2=0.5, op1=ALU.add)
                        P.tt("gpsimd", tA[:], tA[:], KKf[:, sl], ALU.mult, [tA, KKf], [tA])
                        if d == 0:
                            s_e1, in_e1, s_e2, in_e2, s_e1x, in_e1x = 1.0, tPf, -1.0, tPf, 1.0, tDL[:]
                            r_e1, r_e2, r_e1x = tP, tP, tDL
                        else:
                            s_e1, in_e1, s_e2, in_e2, s_e1x, in_e1x = -1.0, tDL[:], 1.0, tDL[:], -1.0, tPf
                            r_e1, r_e2, r_e1x = tDL, tDL, tP
                        P.act(tE[:], in_e1, AF.Exp, [r_e1], [tE], scale=s_e1)
                        P.tt("vector", ARv[:, :, 128:256], tE[:].rearrange("p (n c) -> p n c", n=4), Rb[:, sl].rearrange("p (n c) -> p n c", n=4), ALU.mult, [tE, Rb], [AR])
                        P.act(tE[:], in_e1x, AF.Exp, [r_e1x], [tE], scale=s_e1x)
                        P.stt(ARv[:, :, 0:128], tE[:].rearrange("p (n c) -> p n c", n=4), -1.0, KKf[:, sl].rearrange("p (n c) -> p n c", n=4), ALU.mult, ALU.mult, [tE, KKf], [AR])
                        P.act(tE[:], in_e2, AF.Exp, [r_e2], [tE], scale=s_e2)
                        P.tt("vector", KT[:, sl], tE[:], tK[:], ALU.mult, [tE, tK], [KT])
                        P.tt("gpsimd", BT[:, sl], tE[:], tA[:], ALU.mult, [tE, tA], [BT])
                    P.tt("vector", Ffac[:, 0:NCK - 1], Qd[:, 0:NCK - 1], P63[:, 1:NCK], ALU.add, [Qd, P63], [Ffac])
                    P.act(Ffac[:, 0:NCK - 1], Ffac[:, 0:NCK - 1], AF.Exp, [Ffac], [Ffac])
                    if f"AR{b}_{hp}_{d}" in dbg:
                        dump(f"AR{b}_{hp}_{d}", AR, AR[:], [128, NCK, 256], BF16)
                        dump(f"KT{b}_{hp}_{d}", KT, KT[:], [128, S], BF16)
                        dump(f"BT{b}_{hp}_{d}", BT, BT[:], [128, S], BF16)
                        dump(f"F{b}_{hp}_{d}", Ffac, Ffac[:], [128, NCK])
                    if stop_after <= 3:
                        continue
                    order = list(range(NCK)) if d == 0 else list(range(NCK - 1, -1, -1))
                    hsi = 0
                    for half in range(NCK // HC):
                        chunks = order[HC * half:HC * half + HC]
                        for qi, (src_, dstm) in enumerate(((KT, KTM), (BT, BTM))):
                            pb = bk[2 + qi]
                            for li, n in enumerate(chunks):
                                P.tr(pb.bf[:, 128 * li:128 * li + 128], src_[:, 128 * n:128 * n + 128], ident[:], [src_, ident], [pb])
                            P.copy("vector", dstm[:], pb.bf[:, 0:512].rearrange("p (q c) -> p q c", q=4), [pb], [dstm])
                        for g0 in range(0, HC, 4):
                            insts = [(li, h) for li in range(g0, g0 + 4) for h in range(2)]
                            for ii, (li, h) in enumerate(insts):
                                n = chunks[li]
                                hr = slice(64 * h, 64 * h + 64)
                                slot = 2 * li + h
                                cs = slice(128 * n, 128 * n + 128)
                                pm = bk[ii]
                                P.mm(pm.ap[:, 0:256], BT[hr, cs], AR[hr, n, :], True, True, [BT, AR], [pm])
                                P.mm(pm.ap[:, 256:512], KT[hr, cs], AR[hr, n, :], True, True, [KT, AR], [pm])
                                pa2 = bk[(ii + 4) % 8]
                                P.mm(pa2.ap[:, 0:128], AR[hr, n, 0:128], BT[hr, cs], True, True, [AR, BT], [pa2])
                                a4 = AT4.sub(slot)
                                P.tt("vector", AT4[:, slot, :], pm.ap[:], maskT[d][:], ALU.mult, [pm, maskT[d]], [a4])
                                s0 = stgs[ii][0]
                                P.tt("vector", AFt[ii][:], pa2.ap[:, 0:128], maskA[d][:], ALU.mult, [pa2, maskA[d]], [AFt[ii]])
                                P.tt("vector", s0[:, 256:384], pa2.ap[:, 0:128], mM[d][:], ALU.mult, [pa2, mM[d]], [s0])
                                P.tt("gpsimd", s0[:, 128:256], AT4[:, slot, 0:128], blk32[:], ALU.mult, [a4, blk32], [s0])
                                P.copy("gpsimd", s0[:, 0:128], ident[:], [ident], [s0])
                            for lvl in range(1, 6):
                                for ii, (li, h) in enumerate(insts):
                                    cur = stgs[ii][(lvl - 1) % 2]
                                    nxt = stgs[ii][lvl % 2]
                                    pl = bk[ii]
                                    if lvl <= 4:
                                        P.mm(pl.ap[:, 0:256], cur[:, 256:384], cur[:, 0:256], True, True, [cur], [pl])
                                        P.mm(pl.ap[:, 256:384], cur[:, 128:256], cur[:, 256:384], True, True, [cur], [pl])
                                        P.act(nxt[:, 128:384], pl.ap[:, 128:384], AF.Copy, [pl], [nxt])
                                    else:
                                        P.mm(pl.ap[:, 0:128], cur[:, 256:384], cur[:, 0:128], True, True, [cur], [pl])
                                    P.tt("vector", nxt[:, 0:128], pl.ap[:, 0:128], cur[:, 0:128], ALU.add, [pl, cur], [nxt])
                            for ii, (li, h) in enumerate(insts):
                                fin = stgs[ii][1]
                                pl = bk[ii]
                                P.tr(pl.bf[:, 0:128], fin[:, 0:128], ident[:], [fin, ident], [pl])
                                P.act(fin[:, 128:256], pl.bf[:, 0:128], AF.Copy, [pl], [fin])
                            for ii, (li, h) in enumerate(insts):
                                slot = 2 * li + h
                                fin = stgs[ii][1]
                                pl = bk[ii]
                                P.mm(pl.ap[:, 0:128], AT4[:, slot, 0:128], fin[:, 128:256], True, True, [AT4.sub(slot), fin], [pl])
                                P.mm(pl.ap[:, 128:256], AFt[ii][:], fin[:, 0:128], True, True, [AFt[ii], fin], [pl])
                                P.tt("vector", stgs[ii][0][:, 0:256], pl.ap[:, 0:256], off64[:], ALU.mult, [pl, off64], [stgs[ii][0]])
                            for ii, (li, h) in enumerate(insts):
                                fin = stgs[ii][1]
                                pl = bk[ii]
                                P.mm(pl.ap[:, 256:384], fin[:, 128:256], stgs[ii][0][:, 128:256], True, True, [fin, stgs[ii][0]], [pl])
                                P.mm(pl.ap[:, 384:512], fin[:, 0:128], stgs[ii][0][:, 0:128], True, True, [fin, stgs[ii][0]], [pl])
                                P.tt("vector", T64[ii][:], pl.ap[:, 256:512], fin[:, 0:256], ALU.add, [pl, fin], [T64[ii]])
                            for ii, (li, h) in enumerate(insts):
                                pl = bk[ii]
                                P.mm(pl.ap[:, 0:128], AFt[ii][:], T64[ii][:, 0:128], True, True, [AFt[ii], T64[ii]], [pl])
                                P.tt("vector", stgs[ii][0][:, 0:128], pl.ap[:, 0:128], off128[:], ALU.mult, [pl, off128], [stgs[ii][0]])
                            for ii, (li, h) in enumerate(insts):
                                slot = 2 * li + h
                                pl = bk[ii]
                                P.mm(pl.ap[:, 128:256], T64[ii][:, 128:256], stgs[ii][0][:, 0:128], True, True, [T64[ii], stgs[ii][0]], [pl])
                                P.tt("vector", TT[:, slot, :], pl.ap[:, 128:256], T64[ii][:, 0:128], ALU.add, [pl, T64[ii]], [TT.sub(slot)])
                        if f"TT{b}_{hp}_{d}_{half}" in dbg:
                            dump(f"TT{b}_{hp}_{d}_{half}", TT, TT[:], [128, 2 * HC, 128], BF16)
                            dump(f"AT4{b}_{hp}_{d}_{half}", AT4, AT4[:], [128, 2 * HC, 512], BF16)
                        for li, n in enumerate(chunks):
                            gi = HC * half + li
                            cs = slice(128 * n, 128 * n + 128)
                            first = gi == 0
                            Hc = HS[hsi % 2]
                            Hn = HS[(hsi + 1) % 2]
                            hsi += 1
                            pw_ = bk[0]
                            pu_ = bk[1]
                            py_ = bk[2]
                            pg_ = bk[3]
                            for h in range(2):
                                hr = slice(64 * h, 64 * h + 64)
                                hc = slice(64 * h, 64 * h + 64)
                                slot = 2 * li + h
                                a4 = AT4.sub(slot)
                                if not first:
                                    P.mm(pw_.ap[:, hc], AR[hr, n, 0:128], Hc[hr, :], True, False, [AR, Hc], [pw_])
                                P.mm(pw_.ap[:, hc], AT4[:, slot, 256:384], VTM[:, n, hc], first, True, [a4, VTM], [pw_])
                            last = gi == NCK - 1
                            for h in range(2):
                                hr = slice(64 * h, 64 * h + 64)
                                hc = slice(64 * h, 64 * h + 64)
                                slot = 2 * li + h
                                a4 = AT4.sub(slot)
                                if not last:
                                    if not first:
                                        P.mm(pg_.ap[hr, 0:64], ident[hr, hr], Hc[hr, :], True, False, [ident, Hc], [pg_])
                                    P.mm(pg_.ap[hr, 0:64], KTM[:, li, hc], VTM[:, n, hc], first, False, [KTM, VTM], [pg_])
                                if not first:
                                    P.mm(py_.ap[hr, 0:128], Hc[hr, :], AR[hr, n, 128:256], True, False, [Hc, AR], [py_])
                                P.mm(py_.ap[hr, 0:128], VTM[:, n, hc], AT4[:, slot, 384:512], first, False, [VTM, a4], [py_])
                            P.copy("vector", Wsb[:], pw_.ap[:, 0:128], [pw_], [Wsb])
                            for h in range(2):
                                hc = slice(64 * h, 64 * h + 64)
                                slot = 2 * li + h
                                P.mm(pu_.ap[:, hc], TT[:, slot, :], Wsb[:, hc], True, True, [TT.sub(slot), Wsb], [pu_])
                            P.act(Usb[:], pu_.ap[:, 0:128], AF.Copy, [pu_], [Usb])
                            if not last:
                                for h in range(2):
                                    hr = slice(64 * h, 64 * h + 64)
                                    hc = slice(64 * h, 64 * h + 64)
                                    P.mm(pg_.ap[hr, 0:64], BTM[:, li, hc], Usb[:, hc], False, True, [BTM, Usb], [pg_])
                                fi = n if d == 0 else n - 1
                                P.act(Hn[:], pg_.ap[:, 0:64], AF.Copy, [pg_, Ffac], [Hn], scale=Ffac[:, fi:fi + 1])
                            for h in range(2):
                                hr = slice(64 * h, 64 * h + 64)
                                hc = slice(64 * h, 64 * h + 64)
                                slot = 2 * li + h
                                P.mm(py_.ap[hr, 0:128], Usb[:, hc], AT4[:, slot, 128:256], False, True, [Usb, AT4.sub(slot)], [py_])
                            if d == 0:
                                P.copy("vector", Yf[:, cs], py_.ap[:, 0:128], [py_], [Yf])
                            else:
                                P.tt("vector", Yf[:, cs], Yf[:, cs], py_.ap[:, 0:128], ALU.add, [py_, Yf], [Yf])
                dump(f"Y{b}_{hp}", Yf, Yf[:], [128, S])
                if stop_after <= 4:
                    continue
                for i in range(4):
                    sl = slice(512 * i, 512 * i + 512)
                    pm_ = bk[4 + i % 2]
                    pv_ = bk[6 + i % 2]
                    P.mm(pm_.ap[:], ones_blk[:], Yf[:, sl], True, True, [ones_blk, Yf], [pm_])
                    P.stt(tE[:], pm_.ap[:], -1.0 / 64, Yf[:, sl], ALU.mult, ALU.add, [pm_, Yf], [tE])
                    P.act(tK[:], tE[:], AF.Square, [tE], [tK])
                    P.mm(pv_.ap[:], ones_blk[:], tK[:], True, True, [ones_blk, tK], [pv_])
                    P.act(tK[:], pv_.ap[:], AF.Ln, [pv_, gneps_t], [tK], scale=1.0 / 64, bias=gneps_t[:, 0:1])
                    P.act(tK[:], tK[:], AF.Exp, [tK], [tK], scale=-0.5)
                    P.tt("vector", tE[:], tE[:], tK[:], ALU.mult, [tE, tK], [tE])
                    P.ts("vector", tE[:], tE[:], pv[:, 3, hp:hp + 1], ALU.mult, [tE, pv], [tE], s2=pv[:, 4, hp:hp + 1], op1=ALU.add)
                    P.tt("gpsimd", tK[:], BON[:, sl], Vb[:, sl], ALU.mult, [BON, Vb], [tK])
                    P.tt("vector", tE[:], tE[:], tK[:], ALU.add, [tE, tK], [tE])
                    P.tt("vector", YR[:, hp, sl], tE[:], SG[:, sl], ALU.mult, [tE, SG], [YR])
            dump(f"YR{b}", YR, YR[:], [128, 4, S], BF16)
            if stop_after <= 5:
                continue
            QTb, QT = Kf, Kf[:].bitcast(BF16)[:, 0:3072].rearrange("p (j c) -> p j c", j=4)
            V2b, V2 = Yf, Yf[:].bitcast(BF16)[:, 0:3072].rearrange("p (w k l c) -> p w k l c", w=6, k=2, l=2)
            KAb, KA = AR, AR[:].rearrange("p n c -> p (n c)")[:, 0:1536].rearrange("p (k c) -> p k c", k=2)
            PTb, PT = AR, AR[:].rearrange("p n c -> p (n c)")[:, 1536:2304].rearrange("p (h c) -> p h c", h=2)
            GAb, GA = KT, KT[:].rearrange("p (j c) -> p j c", j=4)
            YAb, YA = BT, BT[:].rearrange("p (j c) -> p j c", j=4)
            MGb, MG = AT4, AT4[:].rearrange("p n c -> p (n c)").rearrange("p (j c) -> p j c", j=8)
            BRr = tA
            otile, ot_ap = VTM, VTM[:].rearrange("p n c -> p (n c)").bitcast(F32)
            P.memset("gpsimd", Yf[:].bitcast(BF16)[:, 0:3072], 0.0, [V2b])
            for hh in range(2):
                pb = bk[hh]
                P.mm(pb.ap[:], sel[:, b, :], garow[:, 512 * hh:512 * hh + 512], True, True, [sel, garow], [pb])
                P.copy("vector", GPb[:, 512 * hh:512 * hh + 512], pb.ap[:], [pb], [GPb])
            for ti in range(4):
                t0 = 512 * ti
                lo = max(t0 - 128, 0)
                hi = min(t0 + 640, S)
                off = lo - (t0 - 128)
                nwin = hi - lo
                for j in range(6):
                    wb = load_w(wsc_in[18 + j])
                    for (c0, cn) in ((0, 384), (384, nwin - 384)):
                        if cn <= 0:
                            continue
                        pb = bk[(j * 2 + (c0 > 0)) % 4]
                        for k in range(8):
                            P.mm(pb.ap[:, 0:cn], wb[:, k, :], hT[:, k, lo + c0:lo + c0 + cn], k == 0, k == 7, [wb, hT], [pb])
                        dstb = QTb if j < 4 else KAb
                        dsta = (QT[:, j, off + c0:off + c0 + cn] if j < 4 else KA[:, j - 4, off + c0:off + c0 + cn])
                        if j % 2:
                            P.act(dsta, pb.ap[:, 0:cn], AF.Copy, [pb], [dstb])
                        else:
                            P.copy("vector", dsta, pb.ap[:, 0:cn], [pb], [dstb])
                for j in range(4):
                    wb = load_w(wsc_in[24 + j])
                    pb = bk[4 + j % 2]
                    for k in range(8):
                        P.mm(pb.ap[:], wb[:, k, :], hT[:, k, t0:t0 + 512], k == 0, k == 7, [wb, hT], [pb])
                    P.act(GA[:, j, :], pb.ap[:], AF.Silu, [pb], [GAb])
                for wbk in range(6):
                    tb0 = t0 - 128 + 128 * wbk
                    if tb0 < 0 or tb0 >= S:
                        continue
                    pb = bk[6 + wbk % 2]
                    for k in range(8):
                        P.mm(pb.ap[:, 0:128], hT[:, k, tb0:tb0 + 128], wv_b[:, k, :], k == 0, k == 7, [hT, wv_b], [pb])
                    for kv in range(2):
                        P.copy("vector", V2[:, wbk, kv, 0, 0:64], pb.ap[:, 64 * kv:64 * kv + 64], [pb], [V2b])
                        P.act(V2[:, wbk, kv, 1, 64:128], pb.ap[:, 64 * kv:64 * kv + 64], AF.Copy, [pb], [V2b])
                for qb in range(4):
                    gq = 4 * ti + qb
                    qc = 128 + 128 * qb
                    slots = [sl_ for sl_ in range(3) if 0 <= gq - 1 + sl_ < NCK]
                    for pr_ in range(4):
                        kv = pr_ // 2
                        for hh in range(2):
                            hr = slice(64 * hh, 64 * hh + 64)
                            ps_ = bk[hh]
                            for sl_ in slots:
                                kc = qc - 128 + 128 * sl_
                                P.mm(ps_.ap[:, 128 * sl_:128 * sl_ + 128], KA[hr, kv, kc:kc + 128], QT[hr, pr_, qc:qc + 128], True, True, [KAb, QTb], [ps_])
                            a_, b_ = 128 * slots[0], 128 * slots[-1] + 128
                            P.act(tE[:, a_:b_], ps_.ap[:, a_:b_], AF.Exp, [ps_], [tE], scale=0.125)
                            P.tt("vector", PT[:, hh, a_:b_], tE[:, a_:b_], ebias[:, 2 * pr_ + hh, :, :].rearrange("p s c -> p (s c)")[:, a_:b_], ALU.mult, [tE, ebias], [PTb])
                        po_ = bk[2 + pr_ % 2]
                        pd_ = bk[4 + pr_ % 2]
                        nmm = 2 * len(slots)
                        idx = 0
                        for hh in range(2):
                            for sl_ in slots:
                                P.mm(po_.ap[:, 0:128], V2[:, qb + sl_, kv, hh, :], PT[:, hh, 128 * sl_:128 * sl_ + 128], idx == 0, idx == nmm - 1, [V2b, PTb], [po_])
                                P.mm(pd_.ap[:, 0:128], onesLR[:, hh, :], PT[:, hh, 128 * sl_:128 * sl_ + 128], idx == 0, idx == nmm - 1, [onesLR, PTb], [pd_])
                                idx += 1
                        P.act(tK[:, 0:128], pd_.ap[:, 0:128], AF.Ln, [pd_, esink], [tK], bias=esink[:, pr_:pr_ + 1])
                        P.act(tK[:, 0:128], tK[:, 0:128], AF.Exp, [tK], [tK], scale=-1.0)
                        P.tt("vector", tK[:, 0:128], tK[:, 0:128], po_.ap[:, 0:128], ALU.mult, [tK, po_], [tK])
                        P.tt("vector", YA[:, pr_, 128 * qb:128 * qb + 128], tK[:, 0:128], GA[:, pr_, 128 * qb:128 * qb + 128], ALU.mult, [tK, GAb], [YAb])
                dump(f"YA{b}_{ti}", YAb, YA, [128, 4, 512], BF16)
                for oc in range(8):
                    wr_ = load_w(wsc_br[oc], 512)
                    pr2 = bk[0 + oc % 2]
                    for k in range(4):
                        P.mm(pr2.ap[:], wr_[:, k, :], YR[:, k, t0:t0 + 512], k == 0, k == 3, [wr_, YR], [pr2])
                    wgr = load_w(wsc_in[28 + oc])
                    pg2 = bk[2 + oc % 2]
                    for k in range(8):
                        P.mm(pg2.ap[:], wgr[:, k, :], hT[:, k, t0:t0 + 512], k == 0, k == 7, [wgr, hT], [pg2])
                    P.act(tE[:], pg2.ap[:], AF.Sigmoid, [pg2], [tE])
                    P.tt("vector", BRr[:], tE[:], pr2.ap[:], ALU.mult, [tE, pr2], [BRr])
                    wa_ = load_w(wsc_br[8 + oc], 512)
                    pa3 = bk[4 + oc % 2]
                    for k in range(4):
                        P.mm(pa3.ap[:], wa_[:, k, :], YA[:, k, :], k == 0, k == 3, [wa_, YAb], [pa3])
                    wga = load_w(wsc_in[36 + oc])
                    pg3 = bk[6 + oc % 2]
                    for k in range(8):
                        P.mm(pg3.ap[:], wga[:, k, :], hT[:, k, t0:t0 + 512], k == 0, k == 7, [wga, hT], [pg3])
                    P.act(tK[:], pg3.ap[:], AF.Sigmoid, [pg3], [tK])
                    P.tt("vector", tK[:], tK[:], pa3.ap[:], ALU.mult, [tK, pa3], [tK])
                    P.tt("gpsimd", MG[:, oc, :], tK[:], BRr[:], ALU.add, [tK, BRr], [MGb])
                for tb in range(4):
                    tok = t0 + 128 * tb
                    xb_ = xt[tb % 2]
                    xa_ = xap[tb % 2]
                    P.dma("sync", lambda e, xa_=xa_, tok=tok, b=b: e.dma_start(out=xa_, in_=x[b, tok:tok + 128, :]), writes=[xb_], key=f"xt{tb % 2}")
                    for hh in range(2):
                        po2 = bk[2 * (tb % 2) + hh]
                        for k in range(8):
                            P.mm(po2.ap[:], MG[:, k, 128 * tb:128 * tb + 128], wout_b[:, k, 512 * hh:512 * hh + 512], k == 0, k == 7, [MGb, wout_b], [po2])
                        P.act(junk[:, 0:512], po2.ap[:], AF.Square, [po2], [junk, st4], accum=st4[:, 4 + hh:5 + hh])
                    P.tt("vector", st4[:, 6:7], st4[:, 4:5], st4[:, 5:6], ALU.add, [st4], [st4])
                    P.act(st4[:, 6:7], st4[:, 6:7], AF.Ln, [st4, eps_t], [st4], scale=1.0 / 1024, bias=eps_t[:, 0:1])
                    P.act(st4[:, 7:8], st4[:, 6:7], AF.Exp, [st4], [st4], scale=-0.5)
                    for hh in range(2):
                        po2 = bk[2 * (tb % 2) + hh]
                        cs2 = slice(512 * hh, 512 * hh + 512)
                        P.stt(ot_ap[:, cs2], po2.ap[:], st4[:, 7:8], GPb[:, cs2], ALU.mult, ALU.mult, [po2, st4, GPb], [otile])
                    P.tt("gpsimd", ot_ap, ot_ap, xa_, ALU.add, [otile, xb_], [otile])
                    P.dma("gpsimd", lambda e, tok=tok, b=b: e.dma_start(out=out[b, tok:tok + 128, :], in_=ot_ap), reads=[otile], key="ost")
        P.final_wait("gpsimd", [VTM] if stop_after > 5 else [hT])
        P.emit()
    return nc, dbg_out


def prep_inputs(inputs, nseq=4, ncores=8):
    f = lambda a: np.ascontiguousarray(np.asarray(a, dtype=np.float32))
    w_in = f(inputs["w_in"])[0]
    W = 512
    cols = []
    cols.append(np.arange(4 * W, 4 * W + 128))
    cols.append(np.arange(4 * W + 128, 4 * W + 256))
    for hp in range(4):
        for j in range(4):
            cols.append(np.arange(j * W + 128 * hp, j * W + 128 * hp + 128))
    A0 = 2304
    for j in range(4):
        cols.append(np.arange(A0 + 128 * j, A0 + 128 * j + 128))
    cols.append(np.concatenate([np.arange(A0 + 512, A0 + 576)] * 2))
    cols.append(np.concatenate([np.arange(A0 + 576, A0 + 640)] * 2))
    for j in range(4):
        cols.append(np.arange(A0 + 768 + 128 * j, A0 + 768 + 128 * j + 128))
    G0 = A0 + 1280
    for j in range(16):
        cols.append(np.arange(G0 + 128 * j, G0 + 128 * j + 128))
    assert len(cols) == NCH
    w_inp = np.stack([w_in[:, c].reshape(8, 128, 128).transpose(1, 0, 2).reshape(128, 1024) for c in cols])
    w_vatt = w_in[:, A0 + 640:A0 + 768].reshape(8, 128, 128).transpose(1, 0, 2).reshape(128, 1024)
    mu = f(inputs["mu_shift"])[0]
    rw_cols = np.concatenate(cols[:18])
    mu_l = mu[:, rw_cols].reshape(2, 18, 128).transpose(2, 0, 1)
    pl = lambda v: f(v).reshape(4, 128).T
    w0 = np.stack([pl(f(inputs["w0"])[0, d]) for d in range(2)], axis=1)
    a0 = np.stack([pl(f(inputs["a0"])[0, d]) for d in range(2)], axis=1)
    sink = f(inputs["sink"])[0]
    sink_l = np.repeat(sink.reshape(4, 2, 1), 64, axis=2).reshape(4, 128).T
    pvec = np.stack([pl(inputs["k_k"][0]), pl(inputs["k_a"][0]), pl(np.asarray(inputs["r_k"])[0].reshape(512)),
                     pl(inputs["gn_w"][0]), pl(inputs["gn_b"][0]), sink_l], axis=1)
    b_ada = f(inputs["b_ada"])[0]
    common = {
        "w_ada": f(np.stack([f(inputs["w_ada"])[0][:, 128 * j:128 * j + 128].reshape(8, 128, 128).transpose(1, 0, 2).reshape(128, 1024) for j in range(24)])),
        "b_ada": f(b_ada.reshape(24, 128).T),
        "b_gate": f(np.tile(b_ada[2048:3072][None, :], (nseq, 1))),
        "gpost_r": f(np.tile(f(inputs["g_post"])[0][None, :], (nseq, 1))),
        "gpre": f(f(inputs["g_pre"])[0].reshape(8, 128).T),
        "w_inp": f(w_inp), "w_vatt": f(w_vatt), "mu": f(mu_l), "w0": f(w0), "a0": f(a0),
        "w_up": f(f(inputs["w_up"])[0].reshape(128, 512)), "a_up": f(f(inputs["a_up"])[0].reshape(128, 512)),
        "pvec": f(pvec),
        "w_brr": f(f(inputs["w_br_rwkv"])[0].reshape(4, 128, 8, 128).transpose(2, 1, 0, 3).reshape(8, 128, 512)),
        "w_bra": f(f(inputs["w_br_att"])[0].reshape(4, 128, 8, 128).transpose(2, 1, 0, 3).reshape(8, 128, 512)),
        "w_outp": f(f(inputs["w_out"])[0].reshape(8, 128, 1024).transpose(1, 0, 2).reshape(128, 8192)),
    }
    xs = f(inputs["x"])
    c = f(inputs["c"])
    maps = []
    for i in range(ncores):
        m = dict(common)
        m["x"] = xs[nseq * i:nseq * i + nseq]
        m["cT"] = f(c[nseq * i:nseq * i + nseq].T.reshape(8, 128, nseq).transpose(1, 0, 2))
        maps.append(m)
    return maps


def kernel(**inputs):
    nc, _ = build(4)
    maps = prep_inputs(inputs, 4, 8)
    res = run_bass_kernel_spmd(nc, maps, core_ids=list(range(8)))
    return np.concatenate([np.asarray(r["out"]) for r in res.results], axis=0).astype(np.float32)
```

```python
import math
import numpy as np
from contextlib import ExitStack
import concourse.bass as bass
import concourse.mybir as mybir
from concourse.bass_utils import run_bass_kernel_spmd

F32 = mybir.dt.float32
BF16 = mybir.dt.bfloat16
I32 = mybir.dt.int32
AF = mybir.ActivationFunctionType
ALU = mybir.AluOpType

ENGS = ["sync", "scalar", "vector", "gpsimd", "tensor"]
S = 2048
NCK = 16
NCH = 44
EXPM05 = math.exp(-0.5)


class Buf:
    def __init__(self, P, t, name, atoms):
        self.P = P
        self.t = t
        self.name = name
        self.own = atoms
        self.kids = {}

    def __getitem__(self, idx):
        return self.t[idx]

    def atoms(self):
        a = list(self.own)
        for k in self.kids.values():
            a += k.own
        return a

    def sub(self, key):
        if key not in self.kids:
            aid = self.P.new_atom()
            st = self.P.state
            st[aid] = {"w": st[self.own[0]]["w"], "r": dict(st[self.own[0]]["r"])}
            self.kids[key] = Buf(self.P, self.t, f"{self.name}.{key}", [aid])
        return self.kids[key]


class Prog:
    def __init__(self, nc, es):
        self.nc = nc
        self.es = es
        self.lists = {e: [] for e in ENGS}
        self.cnt = {e: 0 for e in ENGS}
        self.seen = {e: {} for e in ENGS}
        self.sems = {}
        for e in ENGS:
            self.sems[e] = es.enter_context(nc.semaphore("s_" + e))
        self.dma_cnt = {}
        self.nbuf = 0
        self.state = {}
        self.natom = 0
        self.dbg = {}

    def new_atom(self):
        self.natom += 1
        self.state[self.natom] = {"w": None, "r": {}}
        return self.natom

    def sb(self, shape, dt, name=None):
        self.nbuf += 1
        name = (name or "b") + f"_{self.nbuf}"
        t = self.es.enter_context(self.nc.sbuf_tensor(name, list(shape), dt))
        return Buf(self, t, name, [self.new_atom()])

    def _deps(self, eng, reads, writes):
        waits = {}

        def need(ev):
            if ev is None:
                return
            k, v = ev
            if eng == "tensor" and k == "tensor":
                return
            if self.seen[eng].get(k, 0) >= v:
                return
            waits[k] = max(waits.get(k, 0), v)

        for b in reads:
            for a in b.atoms():
                need(self.state[a]["w"])
        for b in writes:
            for a in b.atoms():
                st = self.state[a]
                need(st["w"])
                for k, v in st["r"].items():
                    need((k, v))
        for k, v in waits.items():
            self.seen[eng][k] = v
        return waits

    def _mark(self, key, c, reads, writes):
        for b in reads:
            for a in b.atoms():
                self.state[a]["r"][key] = c
        for b in writes:
            for a in b.atoms():
                self.state[a]["w"] = (key, c)
                self.state[a]["r"] = {}

    def op(self, eng, fn, reads=(), writes=()):
        waits = self._deps(eng, reads, writes)
        self.cnt[eng] += 1
        self.lists[eng].append((fn, waits, (eng, 1)))
        self._mark(eng, self.cnt[eng], reads, writes)

    def dma(self, eng, fn, reads=(), writes=(), key=None):
        if key not in self.sems:
            self.sems[key] = self.es.enter_context(self.nc.semaphore("s_" + key))
            self.dma_cnt[key] = 0
        waits = self._deps(eng, reads, writes)
        self.dma_cnt[key] += 16
        self.lists[eng].append((fn, waits, (key, 16)))
        self._mark(key, self.dma_cnt[key], reads, writes)

    def final_wait(self, eng, bufs):
        waits = self._deps(eng, bufs, bufs)
        self.lists[eng].append((None, waits, None))

    def emit(self):
        with self.nc.Block() as block:
            def run(engname):
                def f(e):
                    for fn, waits, inc in self.lists[engname]:
                        for k, v in waits.items():
                            e.wait_ge(self.sems[k], v)
                        if fn is None:
                            continue
                        fn(e).then_inc(self.sems[inc[0]], inc[1])
                return f
            block.sync(run("sync"))
            block.scalar(run("scalar"))
            block.vector(run("vector"))
            block.gpsimd(run("gpsimd"))
            block.tensor(run("tensor"))

    def act(self, out, in_, func, r, w, scale=1.0, bias=None, accum=None):
        kw = dict(out=out, in_=in_, func=func, scale=scale)
        if bias is not None:
            kw["bias"] = bias
        if accum is not None:
            kw["accum_out"] = accum
        self.op("scalar", lambda e: e.activation(**kw), r, w)

    def copy(self, eng, out, in_, r, w):
        self.op(eng, lambda e: e.tensor_copy(out=out, in_=in_), r, w)

    def tt(self, eng, out, in0, in1, op, r, w):
        self.op(eng, lambda e: e.tensor_tensor(out=out, in0=in0, in1=in1, op=op), r, w)

    def ts(self, eng, out, in0, s1, op0, r, w, s2=None, op1=None):
        if op1 is None:
            self.op(eng, lambda e: e.tensor_scalar(out=out, in0=in0, scalar1=s1, scalar2=None, op0=op0), r, w)
        else:
            self.op(eng, lambda e: e.tensor_scalar(out=out, in0=in0, scalar1=s1, scalar2=s2, op0=op0, op1=op1), r, w)

    def stt(self, out, in0, scalar, in1, op0, op1, r, w):
        self.op("vector", lambda e: e.scalar_tensor_tensor(out=out, in0=in0, scalar=scalar, in1=in1, op0=op0, op1=op1), r, w)

    def mm(self, out, lhsT, rhs, start, stop, r, w):
        self.op("tensor", lambda e: e.matmul(out, lhsT=lhsT, rhs=rhs, start=start, stop=stop), r, w)

    def tr(self, out, in_, ident, r, w):
        self.op("tensor", lambda e: e.transpose(out=out, in_=in_, identity=ident), r, w)

    def memset(self, eng, ap, val, w):
        self.op(eng, lambda e: e.memset(ap, val), (), w)


def build(nseq=4, dbg=(), stop_after=99):
    nc = bass.Bass("TRN2", target_bir_lowering=False)

    def din(name, shape, dt=F32):
        return nc.dram_tensor(name, list(shape), dt, kind="ExternalInput").ap()

    x = din("x", [nseq, S, 1024])
    cT = din("cT", [128, 8, nseq])
    w_ada = din("w_ada", [24, 128, 1024])
    b_ada = din("b_ada", [128, 24])
    b_gate = din("b_gate", [nseq, 1024])
    gpost_r = din("gpost_r", [nseq, 1024])
    gpre = din("gpre", [128, 8])
    w_inp = din("w_inp", [NCH, 128, 1024])
    w_vatt = din("w_vatt", [128, 1024])
    mu = din("mu", [128, 2, 18])
    w0 = din("w0", [128, 2, 4])
    a0 = din("a0", [128, 2, 4])
    w_up = din("w_up", [128, 512])
    a_up = din("a_up", [128, 512])
    pvec = din("pvec", [128, 6, 4])
    w_brr = din("w_brr", [8, 128, 512])
    w_bra = din("w_bra", [8, 128, 512])
    w_outp = din("w_outp", [128, 8192])
    out = nc.dram_tensor("out", [nseq, S, 1024], F32, kind="ExternalOutput").ap()
    dbg_out = {}

    with ExitStack() as es:
        P = Prog(nc, es)
        psA_t = es.enter_context(nc.psum_tensor("psA", [128, 2048], F32))
        psB_t = es.enter_context(nc.psum_tensor("psB", [128, 2048], F32))
        bank_atoms = [P.new_atom() for _ in range(8)]

        class PsV:
            def __init__(self, b0, n=1):
                self.b0, self.n = b0, n
                self.own = bank_atoms[b0:b0 + n]
                t = psA_t if b0 < 4 else psB_t
                c0 = (b0 % 4) * 512
                self.ap = t[:, c0:c0 + 512 * n]
                self.bf = t[:, c0:c0 + 512 * n].bitcast(BF16)

            def atoms(self):
                return self.own

        bk = [PsV(i) for i in range(8)]
        big = [PsV(0, 4), PsV(4, 4)]

        def dump(name, buf, ap, shape, dt=F32):
            if name not in dbg:
                return
            d = nc.dram_tensor("dbg_" + name, list(shape), dt, kind="ExternalOutput").ap()
            dbg_out[name] = d
            P.dma("gpsimd", lambda e: e.dma_start(out=d, in_=ap), reads=[buf], key="dbg_" + name)
            P.final_wait("gpsimd", [buf])

        TQ = 512
        tA = P.sb([128, TQ], F32, "tA")
        tLW = P.sb([128, TQ], F32, "tLW")
        tP = P.sb([128, 4, 128], F32, "tP")
        tDL = P.sb([128, TQ], F32, "tDL")
        tE = P.sb([128, TQ], F32, "tE")
        tK = P.sb([128, TQ], F32, "tK")
        ones_f = P.sb([128, 128], F32, "ones_f")
        P.memset("gpsimd", ones_f[:], 1.0, [ones_f])
        tmpm = P.sb([128, 128], F32, "tmpm")

        def aff(dst_buf, dst_ap, pattern_step, cm, cmp):
            P.op("gpsimd", lambda e: e.affine_select(out=tmpm[:], in_=ones_f[:], pattern=[[pattern_step, 128]],
                                                     compare_op=cmp, fill=0.0, base=0, channel_multiplier=cm),
                 [ones_f], [tmpm])
            P.copy("gpsimd", dst_ap, tmpm[:], [tmpm], [dst_buf])

        ident = P.sb([128, 128], BF16, "ident")
        aff(ident, ident[:], -1, 1, ALU.is_equal)
        maskT = [P.sb([128, 512], BF16, "maskTf"), P.sb([128, 512], BF16, "maskTb")]
        maskA = [P.sb([128, 128], BF16, "maskAf"), P.sb([128, 128], BF16, "maskAb")]
        SU = (1, -1, ALU.is_gt)
        IU = (1, -1, ALU.is_ge)
        SL = (-1, 1, ALU.is_gt)
        IL = (-1, 1, ALU.is_ge)
        for q, m in enumerate([SU, IU, SU, IU]):
            aff(maskT[0], maskT[0][:, 128 * q:128 * q + 128], *m)
        for q, m in enumerate([SL, IL, SL, IL]):
            aff(maskT[1], maskT[1][:, 128 * q:128 * q + 128], *m)
        aff(maskA[0], maskA[0][:], *SL)
        aff(maskA[1], maskA[1][:], *SU)
        blk32 = P.sb([128, 128], BF16, "blk32")
        off64 = P.sb([128, 256], BF16, "off64")
        off128 = P.sb([128, 128], BF16, "off128")
        for t_ in (blk32, off64, off128):
            P.memset("gpsimd", t_[:], 0.0, [t_])
        for q in range(4):
            P.memset("gpsimd", blk32[32 * q:32 * q + 32, 32 * q:32 * q + 32], 1.0, [blk32])
            q2 = q ^ 1
            for rpt in range(2):
                P.memset("gpsimd", off64[32 * q:32 * q + 32, 128 * rpt + 32 * q2:128 * rpt + 32 * q2 + 32], 1.0, [off64])
        P.memset("gpsimd", off128[0:64, 64:128], 1.0, [off128])
        P.memset("gpsimd", off128[64:128, 0:64], 1.0, [off128])
        mM = [P.sb([128, 128], BF16, "mMf"), P.sb([128, 128], BF16, "mMb")]
        for d_ in range(2):
            P.tt("gpsimd", mM[d_][:], maskA[d_][:], blk32[:], ALU.mult, [maskA[d_], blk32], [mM[d_]])
        ones_blk = P.sb([128, 128], F32, "ones_blk")
        P.memset("gpsimd", ones_blk[:], 1.0, [ones_blk])
        P.memset("gpsimd", ones_blk[0:64, 64:128], 0.0, [ones_blk])
        P.memset("gpsimd", ones_blk[64:128, 0:64], 0.0, [ones_blk])
        onesLR = P.sb([128, 2, 128], BF16, "onesLR")
        P.memset("gpsimd", onesLR[:], 0.0, [onesLR])
        P.memset("gpsimd", onesLR[:, 0, 0:64], 1.0, [onesLR])
        P.memset("gpsimd", onesLR[:, 1, 64:128], 1.0, [onesLR])
        cmask = P.sb([128, 4, 128], BF16, "cmask")
        P.memset("gpsimd", cmask[:], 1.0, [cmask])
        P.memset("gpsimd", cmask[:, :, 0:1], 0.0, [cmask])
        sel = P.sb([nseq, nseq, 128], F32, "sel")
        P.memset("gpsimd", sel[:], 1.0, [sel])
        P.op("gpsimd", lambda e: e.affine_select(out=sel[:], in_=sel[:], pattern=[[-1, nseq], [0, 128]],
                                                 compare_op=ALU.is_equal, fill=0.0, base=0, channel_multiplier=1),
             [sel], [sel])
        ebias = P.sb([128, 8, 3, 128], BF16, "ebias")
        di_ap = tLW[:, 0:384].bitcast(I32).rearrange("p (s c) -> p s c", s=3)
        df_ap = tE[:, 0:384].rearrange("p (s c) -> p s c", s=3)
        et_ap = tK[:, 0:384].rearrange("p (s c) -> p s c", s=3)
        vm_ap = tA[:, 0:256].rearrange("p (s c) -> p s c", s=2)
        P.op("gpsimd", lambda e: e.iota(di_ap[:, 0, :], pattern=[[1, 128]], base=128, channel_multiplier=-1), (), [tLW])
        P.op("gpsimd", lambda e: e.iota(di_ap[:, 1, :], pattern=[[1, 128]], base=0, channel_multiplier=-1), (), [tLW])
        P.op("gpsimd", lambda e: e.iota(di_ap[:, 2, :], pattern=[[-1, 128]], base=128, channel_multiplier=1), (), [tLW])
        P.copy("vector", df_ap, di_ap, [tLW], [tE])
        P.act(df_ap[:, 1, :], df_ap[:, 1, :], AF.Abs, [tE], [tE])
        aff(tA, vm_ap[:, 0, :], *IL)
        aff(tA, vm_ap[:, 1, :], *IU)
        for h in range(8):
            slope = 2.0 ** (-(h + 1))
            P.act(et_ap, df_ap, AF.Exp, [tE], [tK], scale=-slope)
            P.tt("vector", ebias[:, h, 0, :], et_ap[:, 0, :], vm_ap[:, 0, :], ALU.mult, [tK, tA], [ebias])
            P.copy("vector", ebias[:, h, 1, :], et_ap[:, 1, :], [tK], [ebias])
            P.tt("vector", ebias[:, h, 2, :], et_ap[:, 2, :], vm_ap[:, 1, :], ALU.mult, [tK, tA], [ebias])

        def load_small(src, shape, name):
            b = P.sb(shape, F32, name)
            P.dma("sync", lambda e: e.dma_start(out=b[:], in_=src), writes=[b], key="ld_" + name)
            return b

        mu_t = load_small(mu, [128, 2, 18], "mu")
        w0_t = load_small(w0, [128, 2, 4], "w0")
        a0_t = load_small(a0, [128, 2, 4], "a0")
        pv = load_small(pvec, [128, 6, 4], "pv")
        gpre_t = load_small(gpre, [128, 8], "gpre")
        bada_t = load_small(b_ada, [128, 24], "bada")
        cT_t = load_small(cT, [128, 8, nseq], "cT")
        c0_t = P.sb([128, 18], F32, "c0")
        P.tt("vector", c0_t[:], mu_t[:, 0, :], mu_t[:, 1, :], ALU.add, [mu_t], [c0_t])
        P.ts("vector", c0_t[:], c0_t[:], -1.0, ALU.mult, [c0_t], [c0_t], s2=1.0, op1=ALU.add)
        omka = P.sb([128, 4], F32, "omka")
        P.ts("vector", omka[:], pv[:, 1, :], -1.0, ALU.mult, [pv], [omka], s2=1.0, op1=ALU.add)
        a0h = P.sb([128, 2, 4], F32, "a0h")
        w0h = P.sb([128, 2, 4], F32, "w0h")
        P.ts("vector", a0h[:], a0_t[:], 0.5, ALU.mult, [a0_t], [a0h])
        P.ts("vector", w0h[:], w0_t[:], 0.5, ALU.mult, [w0_t], [w0h])
        kah = P.sb([128, 4], F32, "kah")
        omkah = P.sb([128, 4], F32, "omkah")
        P.ts("vector", kah[:], pv[:, 1, :], 0.5, ALU.mult, [pv], [kah])
        P.ts("vector", omkah[:], pv[:, 1, :], -0.5, ALU.mult, [pv], [omkah], s2=1.0, op1=ALU.add)
        esink = P.sb([128, 4], F32, "esink")
        P.act(esink[:], pv[:, 5, :], AF.Exp, [pv], [esink])
        eps_t = P.sb([128, 1], F32, "eps")
        P.memset("vector", eps_t[:], 1e-6, [eps_t])
        gneps_t = P.sb([128, 1], F32, "gneps")
        P.memset("vector", gneps_t[:], 64e-5, [gneps_t])

        hT = P.sb([128, 8, S], BF16, "hT")
        hTf = hT[:].rearrange("p k s -> p (k s)").bitcast(F32)
        NSS = 8
        NW = 3
        wbf = [P.sb([128, 8, 128], BF16, f"wbf{i}") for i in range(NW)]
        wctr = [0]

        def load_cast(dst_buf, dst_ap, src_ap, ncols):
            c = wctr[0]
            wctr[0] += 1
            sb_ = hT.sub(("st", c % NSS))
            sap = hTf[:, 1024 * (c % NSS):1024 * (c % NSS) + ncols]
            P.dma("sync", lambda e: e.dma_start(out=sap, in_=src_ap), writes=[sb_], key=f"wst{c % NSS}")
            if c % 2:
                P.act(dst_ap, sap, AF.Copy, [sb_], [dst_buf])
            else:
                P.copy("vector", dst_ap, sap, [sb_], [dst_buf])

        wsc_in = nc.dram_tensor("wsc_in", [NCH, 128, 1024], BF16).ap()
        wsc_br = nc.dram_tensor("wsc_br", [16, 128, 512], BF16).ap()
        WSC = Buf(P, None, "wsc", [P.new_atom()])
        wbc = [0]
        for ci in range(NCH + 16):
            wb = wbf[wbc[0] % NW]
            wbc[0] += 1
            if ci < NCH:
                srcw, dstw, ncw = w_inp[ci], wsc_in[ci], 1024
            elif ci < NCH + 8:
                srcw, dstw, ncw = w_brr[ci - NCH], wsc_br[ci - NCH], 512
            else:
                srcw, dstw, ncw = w_bra[ci - NCH - 8], wsc_br[ci - NCH], 512
            wflat = wb[:].rearrange("p k c -> p (k c)")[:, 0:ncw]
            load_cast(wb, wflat, srcw, ncw)
            P.dma("sync", lambda e, dstw=dstw, wflat=wflat: e.dma_start(out=dstw, in_=wflat), reads=[wb], writes=[WSC], key="wsc_w")

        def load_w(src_ap, ncols=1024):
            i = wbc[0] % NW
            wbc[0] += 1
            wb = wbf[i]
            P.dma("sync", lambda e: e.dma_start(out=wb[:].rearrange("p k c -> p (k c)")[:, 0:ncols], in_=src_ap), reads=[WSC], writes=[wb], key=f"wbf{i}")
            return wb

        wout_b = P.sb([128, 8, 1024], BF16, "wout_b")
        for k in range(8):
            load_cast(wout_b, wout_b[:, k, :], w_outp[:, 1024 * k:1024 * k + 1024], 1024)
        wv_b = P.sb([128, 8, 128], BF16, "wv_b")
        load_cast(wv_b, wv_b[:].rearrange("p k c -> p (k c)"), w_vatt, 1024)
        wup_b = P.sb([128, 512], BF16, "wupb")
        aup_b = P.sb([128, 512], BF16, "aupb")
        load_cast(wup_b, wup_b[:], w_up, 512)
        load_cast(aup_b, aup_b[:], a_up, 512)

        Kf = P.sb([128, S], F32, "Kf")
        Yf = P.sb([128, S], F32, "Yf")
        bgate_t = Kf
        gpostr_t = Kf
        P.dma("sync", lambda e: e.dma_start(out=Kf[0:nseq, 0:1024], in_=b_gate), writes=[Kf], key="ld_bgate")
        P.dma("sync", lambda e: e.dma_start(out=Kf[0:nseq, 1024:2048], in_=gpost_r), writes=[Kf], key="ld_bgate")
        cond = P.sb([128, 8, nseq], F32, "cond")
        P.act(cond[:], cT_t[:], AF.Silu, [cT_t], [cond])
        adaT = P.sb([128, 24, nseq], F32, "adaT")
        garow = P.sb([nseq, 1024], F32, "garow")
        for jj in range(24):
            i_ = jj % 2
            wa = Yf.sub(i_)
            wa_ap = Yf[:, 1024 * i_:1024 * i_ + 1024]
            P.dma("sync", lambda e, wa_ap=wa_ap, jj=jj: e.dma_start(out=wa_ap, in_=w_ada[jj]), writes=[wa], key=f"wada{i_}")
            pb = bk[jj % 2]
            for k in range(8):
                P.mm(pb.ap[:, 0:nseq], wa_ap[:, 128 * k:128 * k + 128], cond[:, k, :], k == 0, k == 7, [wa, cond], [pb])
            P.act(adaT[:, jj, :], pb.ap[:, 0:nseq], AF.Identity, [pb, bada_t], [adaT], bias=bada_t[:, jj:jj + 1])
            if jj >= 16:
                pr = bk[2 + jj % 2]
                for k in range(8):
                    P.mm(pr.ap[0:nseq, 0:128], cond[:, k, :], wa_ap[:, 128 * k:128 * k + 128], k == 0, k == 7, [wa, cond], [pr])
                c0 = 128 * (jj - 16)
                P.tt("vector", garow[:, c0:c0 + 128], pr.ap[0:nseq, 0:128], bgate_t[0:nseq, c0:c0 + 128], ALU.add, [pr, bgate_t], [garow])
        P.tt("vector", garow[:], garow[:], gpostr_t[0:nseq, 1024:2048], ALU.mult, [garow, gpostr_t], [garow])
        preA = P.sb([128, 8, nseq], F32, "preA")
        P.ts("vector", preA[:], adaT[:, 8:16, :], 1.0, ALU.add, [adaT], [preA])
        for b in range(nseq):
            P.tt("vector", preA[:, :, b], preA[:, :, b], gpre_t[:], ALU.mult, [preA, gpre_t], [preA])
        dump("adaT", adaT, adaT[:], [128, 24, nseq])
        dump("garow", garow, garow[:], [nseq, 1024])

        TW = P.sb([128, S], BF16, "TW")
        AL = P.sb([128, S], BF16, "AL")
        YR = P.sb([128, 4, S], BF16, "YR")
        xn = P.sb([128, 1024], BF16, "xn")
        st4 = P.sb([128, 8], F32, "st4")
        st4b = P.sb([128, 8], F32, "st4b")
        GPb = P.sb([128, 1024], BF16, "GPb")
        Rb = P.sb([128, S], BF16, "Rb")
        Vb = P.sb([128, S], BF16, "Vb")
        KKf = P.sb([128, S], BF16, "KKb")
        SG = P.sb([128, S], BF16, "SG")
        xt = [Rb, Vb]
        xap = [Rb[:].bitcast(F32), Vb[:].bitcast(F32)]
        junk = xn
        BON = P.sb([128, S], BF16, "BON")
        VTM = P.sb([128, NCK, 128], BF16, "VTM")
        AR = P.sb([128, NCK, 256], BF16, "AR")
        KT = P.sb([128, S], BF16, "KT")
        BT = P.sb([128, S], BF16, "BT")
        KTM = P.sb([128, 4, 128], BF16, "KTM")
        BTM = P.sb([128, 4, 128], BF16, "BTM")
        Ffac = P.sb([128, NCK], F32, "Ffac")
        P63 = P.sb([128, NCK], F32, "P63")
        Qd = P.sb([128, NCK], F32, "Qd")
        HC = 4
        AT4 = P.sb([128, 2 * HC, 512], BF16, "AT4")
        TT = P.sb([128, 2 * HC, 128], BF16, "TT")
        stgs = [[P.sb([128, 384], BF16, f"stg{i}_{j}") for j in range(2)] for i in range(8)]
        AFt = [P.sb([128, 128], BF16, f"AF{i}") for i in range(8)]
        T64 = [P.sb([128, 256], BF16, f"T64{i}") for i in range(8)]
        HS = [P.sb([128, 64], BF16, f"HS{i}") for i in range(2)]
        Wsb = P.sb([128, 128], BF16, "Wsb")
        Usb = P.sb([128, 128], BF16, "Usb")

        def proj_full(wb, pbig):
            for i in range(4):
                for k in range(8):
                    P.mm(pbig.ap[:, 512 * i:512 * i + 512], wb[:, k, :], hT[:, k, 512 * i:512 * i + 512], k == 0, k == 7, [wb, hT], [pbig])

        def shift_evac(pbig, ci, dst, dst_ap_fn):
            P.act(dst_ap_fn(0, S), pbig.ap[:, 0:S], AF.Copy, [pbig, c0_t], [dst], scale=c0_t[:, ci:ci + 1])
            P.stt(dst_ap_fn(1, S), pbig.ap[:, 0:S - 1], mu_t[:, 0, ci:ci + 1], dst_ap_fn(1, S), ALU.mult, ALU.add, [pbig, mu_t, dst], [dst])
            P.stt(dst_ap_fn(0, S - 1), pbig.ap[:, 1:S], mu_t[:, 1, ci:ci + 1], dst_ap_fn(0, S - 1), ALU.mult, ALU.add, [pbig, mu_t, dst], [dst])

        pcnt = [0]

        def next_big():
            pcnt[0] += 1
            return big[pcnt[0] % 2]

        for b in range(nseq):
            xn_b = [xn, tLW]
            xn_ap = [xn[:], tLW[:].bitcast(BF16)]
            st_b = [st4, st4b]

            def stats(tb):
                xb_ = xt[tb % 2]
                xa_ = xap[tb % 2]
                s_ = st_b[tb % 2]
                P.dma("sync", lambda e, xa_=xa_, tb=tb, b=b: e.dma_start(out=xa_, in_=x[b, 128 * tb:128 * tb + 128, :]), writes=[xb_], key=f"xt{tb % 2}")
                P.act(tA[:].bitcast(BF16), xa_, AF.Square, [xb_], [tA, s_], accum=s_[:, 0:1])
                P.act(s_[:, 1:2], s_[:, 0:1], AF.Ln, [s_, eps_t], [s_], scale=1.0 / 1024, bias=eps_t[:, 0:1])
                P.act(s_[:, 2:3], s_[:, 1:2], AF.Exp, [s_], [s_], scale=-0.5)

            stats(0)
            for tb in range(16):
                if tb + 1 < 16:
                    stats(tb + 1)
                xb_ = xt[tb % 2]
                xa_ = xap[tb % 2]
                s_ = st_b[tb % 2]
                xnb, xna = xn_b[tb % 2], xn_ap[tb % 2]
                P.ts("vector", xna, xa_, s_[:, 2:3], ALU.mult, [xb_, s_], [xnb])
                pb = bk[tb % 2]
                for k in range(8):
                    P.tr(pb.bf[:, 128 * k:128 * k + 128], xna[:, 128 * k:128 * k + 128], ident[:], [xnb, ident], [pb])
                for k in range(8):
                    P.act(hT[:, k, 128 * tb:128 * tb + 128], pb.bf[:, 128 * k:128 * k + 128], AF.Identity, [pb, preA, adaT], [hT],
                          scale=preA[:, k, b:b + 1], bias=adaT[:, k, b:b + 1])
            dump(f"hT{b}", hT, hT[:], [128, 8, S], BF16)
            if stop_after <= 1:
                continue
            for ci, (dst, fn) in enumerate([(TW, AF.Tanh), (AL, AF.Copy)]):
                wb = load_w(wsc_in[ci])
                pbig = next_big()
                proj_full(wb, pbig)
                tS = Kf
                shift_evac(pbig, ci, tS, lambda a, c: tS[:, a:c])
                P.act(dst[:], tS[:], fn, [tS], [dst])
            dump(f"TW{b}", TW, TW[:], [128, S], BF16)
            dump(f"AL{b}", AL, AL[:], [128, S], BF16)
            if stop_after <= 2:
                continue
            for hp in range(4):
                for j, name in enumerate("rkvg"):
                    ci = 2 + 4 * hp + j
                    wb = load_w(wsc_in[ci])
                    pbig = next_big()
                    proj_full(wb, pbig)
                    if name == "k":
                        shift_evac(pbig, ci, Kf, lambda a, c: Kf[:, a:c])
                    else:
                        tS = Yf
                        shift_evac(pbig, ci, tS, lambda a, c: tS[:, a:c])
                        if name == "r":
                            P.act(Rb[:], tS[:], AF.Copy, [tS], [Rb])
                        elif name == "v":
                            P.act(Vb[:], tS[:], AF.Copy, [tS], [Vb])
                        else:
                            P.act(SG[:], tS[:], AF.Silu, [tS], [SG])
                P.ts("vector", Yf[:], Kf[:], pv[:, 0, hp:hp + 1], ALU.mult, [Kf, pv], [Yf])
                for i in range(4):
                    sl = slice(512 * i, 512 * i + 512)
                    P.act(tE[:], Yf[:, sl], AF.Square, [Yf], [tE])
                    pb = bk[i % 2]
                    P.mm(pb.ap[:], ones_blk[:], tE[:], True, True, [ones_blk, tE], [pb])
                    P.ts("vector", tK[:], pb.ap[:], 1e-24, ALU.max, [pb], [tK])
                    P.act(tK[:], tK[:], AF.Ln, [tK], [tK])
                    P.act(tK[:], tK[:], AF.Exp, [tK], [tK], scale=-0.5)
                    P.tt("vector", KKf[:, sl], Yf[:, sl], tK[:], ALU.mult, [Yf, tK], [KKf])
                dump(f"KK{b}_{hp}", KKf, KKf[:], [128, S], BF16)
                dump(f"R{b}_{hp}", Rb, Rb[:], [128, S], BF16)
                for g4 in range(4):
                    pb = bk[2 + g4 % 2]
                    for q in range(4):
                        n = 4 * g4 + q
                        P.tr(pb.bf[:, 128 * q:128 * q + 128], Vb[:, 128 * n:128 * n + 128], ident[:], [Vb, ident], [pb])
                    P.copy("vector", VTM[:, 4 * g4:4 * g4 + 4, :], pb.bf[:, 0:512].rearrange("p (q c) -> p q c", q=4), [pb], [VTM])
                for d in range(2):
                    rows = slice(64 * d, 64 * d + 64)
                    for i in range(4):
                        sl = slice(512 * i, 512 * i + 512)
                        pa = bk[0 + i % 2]
                        pw = bk[2 + i % 2]
                        P.mm(pa.ap[:], aup_b[rows, 128 * hp:128 * hp + 128], AL[rows, sl], True, True, [aup_b, AL], [pa])
                        P.mm(pw.ap[:], wup_b[rows, 128 * hp:128 * hp + 128], TW[rows, sl], True, True, [wup_b, TW], [pw])
                        P.act(tA[:], pa.ap[:], AF.Tanh, [pa, a0h], [tA], scale=0.5, bias=a0h[:, d, hp:hp + 1])
                        P.act(tLW[:], pw.ap[:], AF.Tanh, [pw, w0h], [tLW], scale=0.5, bias=w0h[:, d, hp:hp + 1])
                        P.ts("vector", tLW[:], tLW[:], -0.5 * EXPM05, ALU.mult, [tLW], [tLW], s2=-0.5 * EXPM05, op1=ALU.add)
                        P.op("vector", lambda e, i=i: e.tensor_tensor_scan(out=tP[:].rearrange("p n c -> p (n c)"),
                                                                          data0=cmask[:].rearrange("p n c -> p (n c)"),
                                                                          data1=tLW[:], initial=0.0, op0=ALU.mult, op1=ALU.add),
                             [cmask, tLW], [tP])
                        P.copy("vector", P63[:, 4 * i:4 * i + 4], tP[:, :, 63], [tP], [P63])
                        P.tt("vector", Qd[:, 4 * i:4 * i + 4], tP[:, :, 127], tP[:, :, 63], ALU.subtract, [tP], [Qd])
                        P.tt("vector", tP[:], tP[:], P63[:, 4 * i:4 * i + 4].unsqueeze(2).to_broadcast([128, 4, 128]), ALU.subtract, [tP, P63], [tP])
                        tPf = tP[:].rearrange("p n c -> p (n c)")
                        P.tt("vector", tDL[:], tPf, tLW[:], ALU.subtract, [tP, tLW], [tDL])
                        ARv = AR[:, 4 * i:4 * i + 4, :]
                        P.ts("vector", tK[:], tA[:], kah[:, hp:hp + 1], ALU.mult, [tA, kah, omkah], [tK], s2=omkah[:, hp:hp + 1], op1=ALU.add)
                        P.tt("vector", tK[:], tK[:], Kf[:, sl], ALU.mult, [tK, Kf], [tK])
                        P.stt(tE[:], tK[:], pv[:, 2, hp:hp + 1], Rb[:, sl], ALU.mult, ALU.mult, [tK, pv, Rb], [tE])
                        pbn = bk[4 + i % 2]
                        P.mm(pbn.ap[:], ones_blk[:], tE[:], True, True, [ones_blk, tE], [pbn])
                        if d == 0:
                            P.copy("vector", BON[:, sl], pbn.ap[:], [pbn], [BON])
                        else:
                            P.tt("vector", BON[:, sl], BON[:, sl], pbn.ap[:], ALU.add, [pbn, BON], [BON])
                        P.ts("gpsimd", tA[:], tA[:], 0.5, ALU.mult, [tA], [tA], s2=0.5, op1=ALU.add)
                        P.tt("gpsimd", tA[:], tA[:], KKf[:, sl], ALU.mult, [tA, KKf], [tA])
                        if d == 0:
                            s_e1, in_e1, s_e2, in_e2, s_e1x, in_e1x = 1.0, tPf, -1.0, tPf, 1.0, tDL[:]
                            r_e1, r_e2, r_e1x = tP, tP, tDL
                        else:
                            s_e1, in_e1, s_e2, in_e2, s_e1x, in_e1x = -1.0, tDL[:], 1.0, tDL[:], -1.0, tPf
                            r_e1, r_e2, r_e1x = tDL, tDL, tP
                        P.act(tE[:], in_e1, AF.Exp, [r_e1], [tE], scale=s_e1)
                        P.tt("vector", ARv[:, :, 128:256], tE[:].rearrange("p (n c) -> p n c", n=4), Rb[:, sl].rearrange("p (n c) -> p n c", n=4), ALU.mult, [tE, Rb], [AR])
                        P.act(tE[:], in_e1x, AF.Exp, [r_e1x], [tE], scale=s_e1x)
                        P.stt(ARv[:, :, 0:128], tE[:].rearrange("p (n c) -> p n c", n=4), -1.0, KKf[:, sl].rearrange("p (n c) -> p n c", n=4), ALU.mult, ALU.mult, [tE, KKf], [AR])
                        P.act(tE[:], in_e2, AF.Exp, [r_e2], [tE], scale=s_e2)
                        P.tt("vector", KT[:, sl], tE[:], tK[:], ALU.mult, [tE, tK], [KT])
                        P.tt("gpsimd", BT[:, sl], tE[:], tA[:], ALU.mult, [tE, tA], [BT])
                    P.tt("vector", Ffac[:, 0:NCK - 1], Qd[:, 0:NCK - 1], P63[:, 1:NCK], ALU.add, [Qd, P63], [Ffac])
                    P.act(Ffac[:, 0:NCK - 1], Ffac[:, 0:NCK - 1], AF.Exp, [Ffac], [Ffac])
                    if f"AR{b}_{hp}_{d}" in dbg:
                        dump(f"AR{b}_{hp}_{d}", AR, AR[:], [128, NCK, 256], BF16)
                        dump(f"KT{b}_{hp}_{d}", KT, KT[:], [128, S], BF16)
                        dump(f"BT{b}_{hp}_{d}", BT, BT[:], [128, S], BF16)
                        dump(f"F{b}_{hp}_{d}", Ffac, Ffac[:], [128, NCK])
                    if stop_after <= 3:
                        continue
                    order = list(range(NCK)) if d == 0 else list(range(NCK - 1, -1, -1))
                    hsi = 0
                    for half in range(NCK // HC):
                        chunks = order[HC * half:HC * half + HC]
                        for qi, (src_, dstm) in enumerate(((KT, KTM), (BT, BTM))):
                            pb = bk[2 + qi]
                            for li, n in enumerate(chunks):
                                P.tr(pb.bf[:, 128 * li:128 * li + 128], src_[:, 128 * n:128 * n + 128], ident[:], [src_, ident], [pb])
                            P.copy("vector", dstm[:], pb.bf[:, 0:512].rearrange("p (q c) -> p q c", q=4), [pb], [dstm])
                        for g0 in range(0, HC, 4):
                            insts = [(li, h) for li in range(g0, g0 + 4) for h in range(2)]
                            for ii, (li, h) in enumerate(insts):
                                n = chunks[li]
                                hr = slice(64 * h, 64 * h + 64)
                                slot = 2 * li + h
                                cs = slice(128 * n, 128 * n + 128)
                                pm = bk[ii]
                                P.mm(pm.ap[:, 0:256], BT[hr, cs], AR[hr, n, :], True, True, [BT, AR], [pm])
                                P.mm(pm.ap[:, 256:512], KT[hr, cs], AR[hr, n, :], True, True, [KT, AR], [pm])
                                pa2 = bk[(ii + 4) % 8]
                                P.mm(pa2.ap[:, 0:128], AR[hr, n, 0:128], BT[hr, cs], True, True, [AR, BT], [pa2])
                                a4 = AT4.sub(slot)
                                P.tt("vector", AT4[:, slot, :], pm.ap[:], maskT[d][:], ALU.mult, [pm, maskT[d]], [a4])
                                s0 = stgs[ii][0]
                                P.tt("vector", AFt[ii][:], pa2.ap[:, 0:128], maskA[d][:], ALU.mult, [pa2, maskA[d]], [AFt[ii]])
                                P.tt("vector", s0[:, 256:384], pa2.ap[:, 0:128], mM[d][:], ALU.mult, [pa2, mM[d]], [s0])
                                P.tt("gpsimd", s0[:, 128:256], AT4[:, slot, 0:128], blk32[:], ALU.mult, [a4, blk32], [s0])
                                P.copy("gpsimd", s0[:, 0:128], ident[:], [ident], [s0])
                            for lvl in range(1, 6):
                                for ii, (li, h) in enumerate(insts):
                                    cur = stgs[ii][(lvl - 1) % 2]
                                    nxt = stgs[ii][lvl % 2]
                                    pl = bk[ii]
                                    if lvl <= 4:
                                        P.mm(pl.ap[:, 0:256], cur[:, 256:384], cur[:, 0:256], True, True, [cur], [pl])
                                        P.mm(pl.ap[:, 256:384], cur[:, 128:256], cur[:, 256:384], True, True, [cur], [pl])
                                        P.act(nxt[:, 128:384], pl.ap[:, 128:384], AF.Copy, [pl], [nxt])
                                    else:
                                        P.mm(pl.ap[:, 0:128], cur[:, 256:384], cur[:, 0:128], True, True, [cur], [pl])
                                    P.tt("vector", nxt[:, 0:128], pl.ap[:, 0:128], cur[:, 0:128], ALU.add, [pl, cur], [nxt])
                            for ii, (li, h) in enumerate(insts):
                                fin = stgs[ii][1]
                                pl = bk[ii]
                                P.tr(pl.bf[:, 0:128], fin[:, 0:128], ident[:], [fin, ident], [pl])
                                P.act(fin[:, 128:256], pl.bf[:, 0:128], AF.Copy, [pl], [fin])
                            for ii, (li, h) in enumerate(insts):
                                slot = 2 * li + h
                                fin = stgs[ii][1]
                                pl = bk[ii]
                                P.mm(pl.ap[:, 0:128], AT4[:, slot, 0:128], fin[:, 128:256], True, True, [AT4.sub(slot), fin], [pl])
                                P.mm(pl.ap[:, 128:256], AFt[ii][:], fin[:, 0:128], True, True, [AFt[ii], fin], [pl])
                                P.tt("vector", stgs[ii][0][:, 0:256], pl.ap[:, 0:256], off64[:], ALU.mult, [pl, off64], [stgs[ii][0]])
                            for ii, (li, h) in enumerate(insts):
                                fin = stgs[ii][1]
                                pl = bk[ii]
                                P.mm(pl.ap[:, 256:384], fin[:, 128:256], stgs[ii][0][:, 128:256], True, True, [fin, stgs[ii][0]], [pl])
                                P.mm(pl.ap[:, 384:512], fin[:, 0:128], stgs[ii][0][:, 0:128], True, True, [fin, stgs[ii][0]], [pl])
                                P.tt("vector", T64[ii][:], pl.ap[:, 256:512], fin[:, 0:256], ALU.add, [pl, fin], [T64[ii]])
                            for ii, (li, h) in enumerate(insts):
                                pl = bk[ii]
                                P.mm(pl.ap[:, 0:128], AFt[ii][:], T64[ii][:, 0:128], True, True, [AFt[ii], T64[ii]], [pl])
                                P.tt("vector", stgs[ii][0][:, 0:128], pl.ap[:, 0:128], off128[:], ALU.mult, [pl, off128], [stgs[ii][0]])
                            for ii, (li, h) in enumerate(insts):
                                slot = 2 * li + h
                                pl = bk[ii]
                                P.mm(pl.ap[:, 128:256], T64[ii][:, 128:256], stgs[ii][0][:, 0:128], True, True, [T64[ii], stgs[ii][0]], [pl])
                                P.tt("vector", TT[:, slot, :], pl.ap[:, 128:256], T64[ii][:, 0:128], ALU.add, [pl, T64[ii]], [TT.sub(slot)])
                        if f"TT{b}_{hp}_{d}_{half}" in dbg:
                            dump(f"TT{b}_{hp}_{d}_{half}", TT, TT[:], [128, 2 * HC, 128], BF16)
                            dump(f"AT4{b}_{hp}_{d}_{half}", AT4, AT4[:], [128, 2 * HC, 512], BF16)
                        for li, n in enumerate(chunks):
                            gi = HC * half + li
                            cs = slice(128 * n, 128 * n + 128)
                            first = gi == 0
                            Hc = HS[hsi % 2]
                            Hn = HS[(hsi + 1) % 2]
                            hsi += 1
                            pw_ = bk[0]
                            pu_ = bk[1]
                            py_ = bk[2]
                            pg_ = bk[3]
                            for h in range(2):
                                hr = slice(64 * h, 64 * h + 64)
                                hc = slice(64 * h, 64 * h + 64)
                                slot = 2 * li + h
                                a4 = AT4.sub(slot)
                                if not first:
                                    P.mm(pw_.ap[:, hc], AR[hr, n, 0:128], Hc[hr, :], True, False, [AR, Hc], [pw_])
                                P.mm(pw_.ap[:, hc], AT4[:, slot, 256:384], VTM[:, n, hc], first, True, [a4, VTM], [pw_])
                            last = gi == NCK - 1
                            for h in range(2):
                                hr = slice(64 * h, 64 * h + 64)
                                hc = slice(64 * h, 64 * h + 64)
                                slot = 2 * li + h
                                a4 = AT4.sub(slot)
                                if not last:
                                    if not first:
                                        P.mm(pg_.ap[hr, 0:64], ident[hr, hr], Hc[hr, :], True, False, [ident, Hc], [pg_])
                                    P.mm(pg_.ap[hr, 0:64], KTM[:, li, hc], VTM[:, n, hc], first, False, [KTM, VTM], [pg_])
                                if not first:
                                    P.mm(py_.ap[hr, 0:128], Hc[hr, :], AR[hr, n, 128:256], True, False, [Hc, AR], [py_])
                                P.mm(py_.ap[hr, 0:128], VTM[:, n, hc], AT4[:, slot, 384:512], first, False, [VTM, a4], [py_])
                            P.copy("vector", Wsb[:], pw_.ap[:, 0:128], [pw_], [Wsb])
                            for h in range(2):
                                hc = slice(64 * h, 64 * h + 64)
                                slot = 2 * li + h
                                P.mm(pu_.ap[:, hc], TT[:, slot, :], Wsb[:, hc], True, True, [TT.sub(slot), Wsb], [pu_])
                            P.act(Usb[:], pu_.ap[:, 0:128], AF.Copy, [pu_], [Usb])
                            if not last:
                                for h in range(2):
                                    hr = slice(64 * h, 64 * h + 64)
                                    hc = slice(64 * h, 64 * h + 64)
                                    P.mm(pg_.ap[hr, 0:64], BTM[:, li, hc], Usb[:, hc], False, True, [BTM, Usb], [pg_])
                                fi = n if d == 0 else n - 1
                                P.act(Hn[:], pg_.ap[:, 0:64], AF.Copy, [pg_, Ffac], [Hn], scale=Ffac[:, fi:fi + 1])
                            for h in range(2):
                                hr = slice(64 * h, 64 * h + 64)
                                hc = slice(64 * h, 64 * h + 64)
                                slot = 2 * li + h
                                P.mm(py_.ap[hr, 0:128], Usb[:, hc], AT4[:, slot, 128:256], False, True, [Usb, AT4.sub(slot)], [py_])
                            if d == 0:
                                P.copy("vector", Yf[:, cs], py_.ap[:, 0:128], [py_], [Yf])
                            else:
                                P.tt("vector", Yf[:, cs], Yf[:, cs], py_.ap[:, 0:128], ALU.add, [py_, Yf], [Yf])
                dump(f"Y{b}_{hp}", Yf, Yf[:], [128, S])
                if stop_after <= 4:
                    continue
                for i in range(4):
                    sl = slice(512 * i, 512 * i + 512)
                    pm_ = bk[4 + i % 2]
                    pv_ = bk[6 + i % 2]
                    P.mm(pm_.ap[:], ones_blk[:], Yf[:, sl], True, True, [ones_blk, Yf], [pm_])
                    P.stt(tE[:], pm_.ap[:], -1.0 / 64, Yf[:, sl], ALU.mult, ALU.add, [pm_, Yf], [tE])
                    P.act(tK[:], tE[:], AF.Square, [tE], [tK])
                    P.mm(pv_.ap[:], ones_blk[:], tK[:], True, True, [ones_blk, tK], [pv_])
                    P.act(tK[:], pv_.ap[:], AF.Ln, [pv_, gneps_t], [tK], scale=1.0 / 64, bias=gneps_t[:, 0:1])
                    P.act(tK[:], tK[:], AF.Exp, [tK], [tK], scale=-0.5)
                    P.tt("vector", tE[:], tE[:], tK[:], ALU.mult, [tE, tK], [tE])
                    P.ts("vector", tE[:], tE[:], pv[:, 3, hp:hp + 1], ALU.mult, [tE, pv], [tE], s2=pv[:, 4, hp:hp + 1], op1=ALU.add)
                    P.tt("gpsimd", tK[:], BON[:, sl], Vb[:, sl], ALU.mult, [BON, Vb], [tK])
                    P.tt("vector", tE[:], tE[:], tK[:], ALU.add, [tE, tK], [tE])
                    P.tt("vector", YR[:, hp, sl], tE[:], SG[:, sl], ALU.mult, [tE, SG], [YR])
            dump(f"YR{b}", YR, YR[:], [128, 4, S], BF16)
            if stop_after <= 5:
                continue
            QTb, QT = Kf, Kf[:].bitcast(BF16)[:, 0:3072].rearrange("p (j c) -> p j c", j=4)
            V2b, V2 = Yf, Yf[:].bitcast(BF16)[:, 0:3072].rearrange("p (w k l c) -> p w k l c", w=6, k=2, l=2)
            KAb, KA = AR, AR[:].rearrange("p n c -> p (n c)")[:, 0:1536].rearrange("p (k c) -> p k c", k=2)
            PTb, PT = AR, AR[:].rearrange("p n c -> p (n c)")[:, 1536:2304].rearrange("p (h c) -> p h c", h=2)
            GAb, GA = KT, KT[:].rearrange("p (j c) -> p j c", j=4)
            YAb, YA = BT, BT[:].rearrange("p (j c) -> p j c", j=4)
            MGb, MG = AT4, AT4[:].rearrange("p n c -> p (n c)").rearrange("p (j c) -> p j c", j=8)
            BRr = tA
            otile, ot_ap = VTM, VTM[:].rearrange("p n c -> p (n c)").bitcast(F32)
            P.memset("gpsimd", Yf[:].bitcast(BF16)[:, 0:3072], 0.0, [V2b])
            for hh in range(2):
                pb = bk[hh]
                P.mm(pb.ap[:], sel[:, b, :], garow[:, 512 * hh:512 * hh + 512], True, True, [sel, garow], [pb])
                P.copy("vector", GPb[:, 512 * hh:512 * hh + 512], pb.ap[:], [pb], [GPb])
            for ti in range(4):
                t0 = 512 * ti
                lo = max(t0 - 128, 0)
                hi = min(t0 + 640, S)
                off = lo - (t0 - 128)
                nwin = hi - lo
                for j in range(6):
                    wb = load_w(wsc_in[18 + j])
                    for (c0, cn) in ((0, 384), (384, nwin - 384)):
                        if cn <= 0:
                            continue
                        pb = bk[(j * 2 + (c0 > 0)) % 4]
                        for k in range(8):
                            P.mm(pb.ap[:, 0:cn], wb[:, k, :], hT[:, k, lo + c0:lo + c0 + cn], k == 0, k == 7, [wb, hT], [pb])
                        dstb = QTb if j < 4 else KAb
                        dsta = (QT[:, j, off + c0:off + c0 + cn] if j < 4 else KA[:, j - 4, off + c0:off + c0 + cn])
                        if j % 2:
                            P.act(dsta, pb.ap[:, 0:cn], AF.Copy, [pb], [dstb])
                        else:
                            P.copy("vector", dsta, pb.ap[:, 0:cn], [pb], [dstb])
                for j in range(4):
                    wb = load_w(wsc_in[24 + j])
                    pb = bk[4 + j % 2]
                    for k in range(8):
                        P.mm(pb.ap[:], wb[:, k, :], hT[:, k, t0:t0 + 512], k == 0, k == 7, [wb, hT], [pb])
                    P.act(GA[:, j, :], pb.ap[:], AF.Silu, [pb], [GAb])
                for wbk in range(6):
                    tb0 = t0 - 128 + 128 * wbk
                    if tb0 < 0 or tb0 >= S:
                        continue
                    pb = bk[6 + wbk % 2]
                    for k in range(8):
                        P.mm(pb.ap[:, 0:128], hT[:, k, tb0:tb0 + 128], wv_b[:, k, :], k == 0, k == 7, [hT, wv_b], [pb])
                    for kv in range(2):
                        P.copy("vector", V2[:, wbk, kv, 0, 0:64], pb.ap[:, 64 * kv:64 * kv + 64], [pb], [V2b])
                        P.act(V2[:, wbk, kv, 1, 64:128], pb.ap[:, 64 * kv:64 * kv + 64], AF.Copy, [pb], [V2b])
                for qb in range(4):
                    gq = 4 * ti + qb
                    qc = 128 + 128 * qb
                    slots = [sl_ for sl_ in range(3) if 0 <= gq - 1 + sl_ < NCK]
                    for pr_ in range(4):
                        kv = pr_ // 2
                        for hh in range(2):
                            hr = slice(64 * hh, 64 * hh + 64)
                            ps_ = bk[hh]
                            for sl_ in slots:
                                kc = qc - 128 + 128 * sl_
                                P.mm(ps_.ap[:, 128 * sl_:128 * sl_ + 128], KA[hr, kv, kc:kc + 128], QT[hr, pr_, qc:qc + 128], True, True, [KAb, QTb], [ps_])
                            a_, b_ = 128 * slots[0], 128 * slots[-1] + 128
                            P.act(tE[:, a_:b_], ps_.ap[:, a_:b_], AF.Exp, [ps_], [tE], scale=0.125)
                            P.tt("vector", PT[:, hh, a_:b_], tE[:, a_:b_], ebias[:, 2 * pr_ + hh, :, :].rearrange("p s c -> p (s c)")[:, a_:b_], ALU.mult, [tE, ebias], [PTb])
                        po_ = bk[2 + pr_ % 2]
                        pd_ = bk[4 + pr_ % 2]
                        nmm = 2 * len(slots)
                        idx = 0
                        for hh in range(2):
                            for sl_ in slots:
                                P.mm(po_.ap[:, 0:128], V2[:, qb + sl_, kv, hh, :], PT[:, hh, 128 * sl_:128 * sl_ + 128], idx == 0, idx == nmm - 1, [V2b, PTb], [po_])
                                P.mm(pd_.ap[:, 0:128], onesLR[:, hh, :], PT[:, hh, 128 * sl_:128 * sl_ + 128], idx == 0, idx == nmm - 1, [onesLR, PTb], [pd_])
                                idx += 1
                        P.act(tK[:, 0:128], pd_.ap[:, 0:128], AF.Ln, [pd_, esink], [tK], bias=esink[:, pr_:pr_ + 1])
                        P.act(tK[:, 0:128], tK[:, 0:128], AF.Exp, [tK], [tK], scale=-1.0)
                        P.tt("vector", tK[:, 0:128], tK[:, 0:128], po_.ap[:, 0:128], ALU.mult, [tK, po_], [tK])
                        P.tt("vector", YA[:, pr_, 128 * qb:128 * qb + 128], tK[:, 0:128], GA[:, pr_, 128 * qb:128 * qb + 128], ALU.mult, [tK, GAb], [YAb])
                dump(f"YA{b}_{ti}", YAb, YA, [128, 4, 512], BF16)
                for oc in range(8):
                    wr_ = load_w(wsc_br[oc], 512)
                    pr2 = bk[0 + oc % 2]
                    for k in range(4):
                        P.mm(pr2.ap[:], wr_[:, k, :], YR[:, k, t0:t0 + 512], k == 0, k == 3, [wr_, YR], [pr2])
                    wgr = load_w(wsc_in[28 + oc])
                    pg2 = bk[2 + oc % 2]
                    for k in range(8):
                        P.mm(pg2.ap[:], wgr[:, k, :], hT[:, k, t0:t0 + 512], k == 0, k == 7, [wgr, hT], [pg2])
                    P.act(tE[:], pg2.ap[:], AF.Sigmoid, [pg2], [tE])
                    P.tt("vector", BRr[:], tE[:], pr2.ap[:], ALU.mult, [tE, pr2], [BRr])
                    wa_ = load_w(wsc_br[8 + oc], 512)
                    pa3 = bk[4 + oc % 2]
                    for k in range(4):
                        P.mm(pa3.ap[:], wa_[:, k, :], YA[:, k, :], k == 0, k == 3, [wa_, YAb], [pa3])
                    wga = load_w(wsc_in[36 + oc])
                    pg3 = bk[6 + oc % 2]
                    for k in range(8):
                        P.mm(pg3.ap[:], wga[:, k, :], hT[:, k, t0:t0 + 512], k == 0, k == 7, [wga, hT], [pg3])
                    P.act(tK[:], pg3.ap[:], AF.Sigmoid, [pg3], [tK])
                    P.tt("vector", tK[:], tK[:], pa3.ap[:], ALU.mult, [tK, pa3], [tK])
                    P.tt("gpsimd", MG[:, oc, :], tK[:], BRr[:], ALU.add, [tK, BRr], [MGb])
                for tb in range(4):
                    tok = t0 + 128 * tb
                    xb_ = xt[tb % 2]
                    xa_ = xap[tb % 2]
                    P.dma("sync", lambda e, xa_=xa_, tok=tok, b=b: e.dma_start(out=xa_, in_=x[b, tok:tok + 128, :]), writes=[xb_], key=f"xt{tb % 2}")
                    for hh in range(2):
                        po2 = bk[2 * (tb % 2) + hh]
                        for k in range(8):
                            P.mm(po2.ap[:], MG[:, k, 128 * tb:128 * tb + 128], wout_b[:, k, 512 * hh:512 * hh + 512], k == 0, k == 7, [MGb, wout_b], [po2])
                        P.act(junk[:, 0:512], po2.ap[:], AF.Square, [po2], [junk, st4], accum=st4[:, 4 + hh:5 + hh])
                    P.tt("vector", st4[:, 6:7], st4[:, 4:5], st4[:, 5:6], ALU.add, [st4], [st4])
                    P.act(st4[:, 6:7], st4[:, 6:7], AF.Ln, [st4, eps_t], [st4], scale=1.0 / 1024, bias=eps_t[:, 0:1])
                    P.act(st4[:, 7:8], st4[:, 6:7], AF.Exp, [st4], [st4], scale=-0.5)
                    for hh in range(2):
                        po2 = bk[2 * (tb % 2) + hh]
                        cs2 = slice(512 * hh, 512 * hh + 512)
                        P.stt(ot_ap[:, cs2], po2.ap[:], st4[:, 7:8], GPb[:, cs2], ALU.mult, ALU.mult, [po2, st4, GPb], [otile])
                    P.tt("gpsimd", ot_ap, ot_ap, xa_, ALU.add, [otile, xb_], [otile])
                    P.dma("gpsimd", lambda e, tok=tok, b=b: e.dma_start(out=out[b, tok:tok + 128, :], in_=ot_ap), reads=[otile], key="ost")
        P.final_wait("gpsimd", [VTM] if stop_after > 5 else [hT])
        P.emit()
    return nc, dbg_out


def prep_inputs(inputs, nseq=4, ncores=8):
    f = lambda a: np.ascontiguousarray(np.asarray(a, dtype=np.float32))
    w_in = f(inputs["w_in"])[0]
    W = 512
    cols = []
    cols.append(np.arange(4 * W, 4 * W + 128))
    cols.append(np.arange(4 * W + 128, 4 * W + 256))
    for hp in range(4):
        for j in range(4):
            cols.append(np.arange(j * W + 128 * hp, j * W + 128 * hp + 128))
    A0 = 2304
    for j in range(4):
        cols.append(np.arange(A0 + 128 * j, A0 + 128 * j + 128))
    cols.append(np.concatenate([np.arange(A0 + 512, A0 + 576)] * 2))
    cols.append(np.concatenate([np.arange(A0 + 576, A0 + 640)] * 2))
    for j in range(4):
        cols.append(np.arange(A0 + 768 + 128 * j, A0 + 768 + 128 * j + 128))
    G0 = A0 + 1280
    for j in range(16):
        cols.append(np.arange(G0 + 128 * j, G0 + 128 * j + 128))
    assert len(cols) == NCH
    w_inp = np.stack([w_in[:, c].reshape(8, 128, 128).transpose(1, 0, 2).reshape(128, 1024) for c in cols])
    w_vatt = w_in[:, A0 + 640:A0 + 768].reshape(8, 128, 128).transpose(1, 0, 2).reshape(128, 1024)
    mu = f(inputs["mu_shift"])[0]
    rw_cols = np.concatenate(cols[:18])
    mu_l = mu[:, rw_cols].reshape(2, 18, 128).transpose(2, 0, 1)
    pl = lambda v: f(v).reshape(4, 128).T
    w0 = np.stack([pl(f(inputs["w0"])[0, d]) for d in range(2)], axis=1)
    a0 = np.stack([pl(f(inputs["a0"])[0, d]) for d in range(2)], axis=1)
    sink = f(inputs["sink"])[0]
    sink_l = np.repeat(sink.reshape(4, 2, 1), 64, axis=2).reshape(4, 128).T
    pvec = np.stack([pl(inputs["k_k"][0]), pl(inputs["k_a"][0]), pl(np.asarray(inputs["r_k"])[0].reshape(512)),
                     pl(inputs["gn_w"][0]), pl(inputs["gn_b"][0]), sink_l], axis=1)
    b_ada = f(inputs["b_ada"])[0]
    common = {
        "w_ada": f(np.stack([f(inputs["w_ada"])[0][:, 128 * j:128 * j + 128].reshape(8, 128, 128).transpose(1, 0, 2).reshape(128, 1024) for j in range(24)])),
        "b_ada": f(b_ada.reshape(24, 128).T),
        "b_gate": f(np.tile(b_ada[2048:3072][None, :], (nseq, 1))),
        "gpost_r": f(np.tile(f(inputs["g_post"])[0][None, :], (nseq, 1))),
        "gpre": f(f(inputs["g_pre"])[0].reshape(8, 128).T),
        "w_inp": f(w_inp), "w_vatt": f(w_vatt), "mu": f(mu_l), "w0": f(w0), "a0": f(a0),
        "w_up": f(f(inputs["w_up"])[0].reshape(128, 512)), "a_up": f(f(inputs["a_up"])[0].reshape(128, 512)),
        "pvec": f(pvec),
        "w_brr": f(f(inputs["w_br_rwkv"])[0].reshape(4, 128, 8, 128).transpose(2, 1, 0, 3).reshape(8, 128, 512)),
        "w_bra": f(f(inputs["w_br_att"])[0].reshape(4, 128, 8, 128).transpose(2, 1, 0, 3).reshape(8, 128, 512)),
        "w_outp": f(f(inputs["w_out"])[0].reshape(8, 128, 1024).transpose(1, 0, 2).reshape(128, 8192)),
    }
    xs = f(inputs["x"])
    c = f(inputs["c"])
    maps = []
    for i in range(ncores):
        m = dict(common)
        m["x"] = xs[nseq * i:nseq * i + nseq]
        m["cT"] = f(c[nseq * i:nseq * i + nseq].T.reshape(8, 128, nseq).transpose(1, 0, 2))
        maps.append(m)
    return maps


def kernel(**inputs):
    nc, _ = build(4)
    maps = prep_inputs(inputs, 4, 8)
    res = run_bass_kernel_spmd(nc, maps, core_ids=list(range(8)))
    return np.concatenate([np.asarray(r["out"]) for r in res.results], axis=0).astype(np.float32)
```
